# Optimizing a Trainium2 kernel written in Bass

```python
import math
import jax, jax.numpy as jnp
from jax import lax
import numpy as np


D_MODEL = 2048
BATCH = 32
SEQ = 256
DEPTH = 2
DEC_BATCH = 8
DEC_SEQ = 2048
PAST_LEN = 512

GRID_W = 64
D_RNN = D_MODEL
LRU_BLOCKS = 16
LRU_BLOCK = D_RNN // LRU_BLOCKS
CONV_W = 4
CONV_PAD_LEFT = 2
LRU_C = 8.0
DA_HEADS = 8
DA_QK_DIM = 64
DA_V_DIM = 128
ROT_AXIS_DIM = DA_QK_DIM // 2
ROPE_BASE = 10000.0
D_FF = ((8 * D_MODEL // 3 + 127) // 128) * 128
N_MOD = 9
Q_BLOCK = 128
NORM_EPS = 1e-6
QK_COLS = DA_HEADS * 2 * DA_QK_DIM
V_COLS = DA_HEADS * DA_V_DIM
IN_COLS = 2 * D_RNN + 2 * QK_COLS + V_COLS + 2 * D_MODEL
IN_SPLITS = (D_RNN, 2 * D_RNN, 2 * D_RNN + QK_COLS, 2 * D_RNN + 2 * QK_COLS, 2 * D_RNN + 2 * QK_COLS + V_COLS)

kernel_name = 'hybrid_rglru_diffattn_prefix_dit_step'


def rms_norm(x, g):
    xf = x.astype(jnp.float32)
    y = xf * lax.rsqrt(jnp.mean(xf * xf, axis=-1, keepdims=True) + NORM_EPS)
    return y.astype(x.dtype) * g


def swiglu(h, w1, w2):
    g, u = jnp.split(h @ w1, 2, axis=-1)
    return (jax.nn.silu(g) * u) @ w2


def axial_rope_tables(n_tokens):
    n_rows = n_tokens // GRID_W
    t = jnp.arange(n_rows * GRID_W)
    row = (t // GRID_W).astype(jnp.float32)
    col = (t % GRID_W).astype(jnp.float32)
    freqs = 1.0 / (ROPE_BASE ** (jnp.arange(0, ROT_AXIS_DIM, 2, dtype=jnp.float32) / ROT_AXIS_DIM))
    ang = jnp.stack([row[:, None] * freqs, col[:, None] * freqs], axis=1)
    return jnp.cos(ang), jnp.sin(ang)


def apply_axial_rope(x, cos, sin):
    shp = x.shape
    xa = x.reshape(shp[:-1] + (2, 2, ROT_AXIS_DIM // 2))
    c = cos[None, :, None, None].astype(x.dtype)
    s = sin[None, :, None, None].astype(x.dtype)
    x1, x2 = xa[..., 0, :], xa[..., 1, :]
    out = jnp.stack([x1 * c - x2 * s, x2 * c + x1 * s], axis=-2)
    return out.reshape(shp)


def diff_attention(q, k, v, lam):
    b, tq = q.shape[0], q.shape[1]
    nb = tq // Q_BLOCK
    qb = jnp.moveaxis(q.reshape(b, nb, Q_BLOCK, DA_HEADS, 2, DA_QK_DIM), 1, 0)
    scale = DA_QK_DIM ** -0.5
    kf = k.astype(jnp.float32)

    def one_block(qblk):
        s = jnp.einsum('bqhmd,bkhmd->bhmqk', qblk.astype(jnp.float32), kf) * scale
        p = jax.nn.softmax(s, axis=-1)
        p_diff = p[:, :, 0] - lam * p[:, :, 1]
        return jnp.einsum('bhqk,bkhd->bqhd', p_diff.astype(v.dtype), v)

    o = lax.map(one_block, qb)
    return jnp.moveaxis(o, 0, 1).reshape(b, tq, DA_HEADS, DA_V_DIM)


def block_diag(x, w, b):
    xs = x.reshape(x.shape[:-1] + (LRU_BLOCKS, LRU_BLOCK))
    return jnp.einsum('btnd,nde->btne', xs, w).reshape(x.shape) + b


def centred_depthwise_conv(x, w, b):
    t = x.shape[1]
    xp = jnp.pad(x, ((0, 0), (CONV_PAD_LEFT, CONV_W - 1 - CONV_PAD_LEFT), (0, 0)))
    return sum(w[j] * xp[:, j:j + t] for j in range(CONV_W)) + b


def rglru_coeffs(y, wa, ba, wx, bx, lam_param):
    gate_a = jax.nn.sigmoid(block_diag(y, wa, ba)).astype(jnp.float32)
    gate_x = jax.nn.sigmoid(block_diag(y, wx, bx)).astype(jnp.float32)
    log_a = -LRU_C * gate_a * jax.nn.softplus(-lam_param.astype(jnp.float32))
    a = jnp.exp(log_a)
    mult = jnp.sqrt(-jnp.expm1(2.0 * log_a))
    return a, mult * gate_x * y.astype(jnp.float32)


def _combine(left, right):
    a1, b1 = left
    a2, b2 = right
    return a1 * a2, a2 * b1 + b2


def linear_scan(a, bterm, h0, reverse):
    a_cum, b_cum = lax.associative_scan(_combine, (a, bterm), axis=1, reverse=reverse)
    return a_cum * h0.astype(jnp.float32)[:, None, :] + b_cum


def mixer(h, lp, l, ctx_k, ctx_v, ctx_state):
    latent = ctx_k is not None
    bsz, t, _ = h.shape
    x_r, g_r, q, k, v, g_m = jnp.split(h @ lp['w_in'], IN_SPLITS, axis=-1)
    y = centred_depthwise_conv(x_r, lp['conv_w'], lp['conv_b'])
    a_f, b_f = rglru_coeffs(y, lp['lru_wa'][0], lp['lru_ba'][0], lp['lru_wx'][0], lp['lru_bx'][0], lp['lru_lambda'][0])
    a_b, b_b = rglru_coeffs(y, lp['lru_wa'][1], lp['lru_ba'][1], lp['lru_wx'][1], lp['lru_bx'][1], lp['lru_lambda'][1])
    if latent:
        h0_f, h0_b = ctx_state[:, 0], ctx_state[:, 1]
    else:
        h0_f = jnp.zeros((bsz, D_RNN), jnp.float32)
        h0_b = h0_f
    hf = linear_scan(a_f, b_f, h0_f, False)
    hb = linear_scan(a_b, b_b, h0_b, True)
    rec = (hf + hb).astype(h.dtype) * jax.nn.gelu(g_r)
    q = q.reshape(bsz, t, DA_HEADS, 2, DA_QK_DIM)
    k = k.reshape(bsz, t, DA_HEADS, 2, DA_QK_DIM)
    v = v.reshape(bsz, t, DA_HEADS, DA_V_DIM)
    lam_init = 0.8 - 0.6 * math.exp(-0.3 * l)
    lam = (jnp.exp(jnp.sum(lp['lam_q1'].astype(jnp.float32) * lp['lam_k1'].astype(jnp.float32)))
           - jnp.exp(jnp.sum(lp['lam_q2'].astype(jnp.float32) * lp['lam_k2'].astype(jnp.float32))) + lam_init)
    if latent:
        cos, sin = axial_rope_tables(t)
        q_r = apply_axial_rope(q, cos, sin)
        k_r = apply_axial_rope(k, cos, sin)
        keys = jnp.concatenate([ctx_k.reshape(bsz, -1, DA_HEADS, 2, DA_QK_DIM), k_r], axis=1)
        vals = jnp.concatenate([ctx_v, v], axis=1)
        o = diff_attention(q_r, keys, vals, lam)
    else:
        o = diff_attention(q, k, v, lam)
    o = (rms_norm(o, lp['attn_subln']) * (1.0 - lam_init)).reshape(bsz, t, V_COLS)
    gm1, gm2 = jnp.split(jax.nn.sigmoid(g_m), 2, axis=-1)
    out = (gm1 * (rec @ lp['p_lru']) + gm2 * (o @ lp['p_attn'])) @ lp['w_out']
    if latent:
        return out, None
    ctx = (k.reshape(bsz, t, DA_HEADS, 2 * DA_QK_DIM), v,
           jnp.stack([hf[:, -1], hb[:, 0]], axis=1).astype(h.dtype))
    return out, ctx


def trunk_layer(x, mod, lp, l, ctx_k, ctx_v, ctx_state):
    sh1, sc1, gt1, sh2, sc2, gt2, sh3, sc3, gt3 = jnp.split(mod, N_MOD, axis=-1)
    h = rms_norm(x, lp['g_pre'][0]) * (1 + sc1) + sh1
    x = x + 0.5 * gt1 * rms_norm(swiglu(h, lp['ffn_w1'][0], lp['ffn_w2'][0]), lp['g_post'][0])
    h = rms_norm(x, lp['g_pre'][1]) * (1 + sc2) + sh2
    m, ctx = mixer(h, lp, l, ctx_k, ctx_v, ctx_state)
    x = x + gt2 * rms_norm(m, lp['g_post'][1])
    h = rms_norm(x, lp['g_pre'][2]) * (1 + sc3) + sh3
    x = x + 0.5 * gt3 * rms_norm(swiglu(h, lp['ffn_w1'][1], lp['ffn_w2'][1]), lp['g_post'][2])
    return x, ctx


def setup_inputs(seed: int = 0) -> dict:
    key = jax.random.key(seed)
    ks = jax.random.split(key, 32)
    f32 = jnp.float32

    def nrm(k, shape, scale):
        return jax.random.normal(k, shape, f32) * scale

    u = jax.random.uniform(ks[31], (DEPTH, 2, D_RNN), f32, 0.9, 0.999)
    a_base = u ** (1.0 / LRU_C)
    lru_lambda = jnp.log(a_base) - jnp.log1p(-a_base)
    return {
        'x_prompt': nrm(ks[0], (BATCH, SEQ, D_MODEL), 1.0),
        'x_sample': nrm(ks[1], (DEC_BATCH, DEC_SEQ, D_MODEL), 1.0),
        'c': nrm(ks[2], (DEC_BATCH, D_MODEL), 1.0),
        'cache_k': nrm(ks[3], (DEC_BATCH, DEPTH, PAST_LEN, DA_HEADS, 2 * DA_QK_DIM), 1.0),
        'cache_v': nrm(ks[4], (DEC_BATCH, DEPTH, PAST_LEN, DA_HEADS, DA_V_DIM), 1.0),
        'state_lru': nrm(ks[5], (DEC_BATCH, DEPTH, 2, D_RNN), 0.5),
        'c_ctx': nrm(ks[6], (D_MODEL,), 1.0),
        'w_mod': nrm(ks[7], (DEPTH, D_MODEL, N_MOD * D_MODEL), 0.5 * D_MODEL ** -0.5),
        'b_mod': nrm(ks[8], (DEPTH, N_MOD * D_MODEL), 0.02),
        'g_pre': 1.0 + nrm(ks[9], (DEPTH, 3, D_MODEL), 0.05),
        'g_post': 1.0 + nrm(ks[10], (DEPTH, 3, D_MODEL), 0.05),
        'ffn_w1': nrm(ks[11], (DEPTH, 2, D_MODEL, 2 * D_FF), D_MODEL ** -0.5),
        'ffn_w2': nrm(ks[12], (DEPTH, 2, D_FF, D_MODEL), D_FF ** -0.5),
        'w_in': nrm(ks[13], (DEPTH, D_MODEL, IN_COLS), D_MODEL ** -0.5),
        'conv_w': nrm(ks[14], (DEPTH, CONV_W, D_RNN), CONV_W ** -0.5),
        'conv_b': nrm(ks[15], (DEPTH, D_RNN), 0.02),
        'lru_wa': nrm(ks[16], (DEPTH, 2, LRU_BLOCKS, LRU_BLOCK, LRU_BLOCK), LRU_BLOCK ** -0.5),
        'lru_ba': nrm(ks[17], (DEPTH, 2, D_RNN), 0.02),
        'lru_wx': nrm(ks[18], (DEPTH, 2, LRU_BLOCKS, LRU_BLOCK, LRU_BLOCK), LRU_BLOCK ** -0.5),
        'lru_bx': nrm(ks[19], (DEPTH, 2, D_RNN), 0.02),
        'lru_lambda': lru_lambda,
        'lam_q1': nrm(ks[20], (DEPTH, DA_QK_DIM), 0.1),
        'lam_k1': nrm(ks[21], (DEPTH, DA_QK_DIM), 0.1),
        'lam_q2': nrm(ks[22], (DEPTH, DA_QK_DIM), 0.1),
        'lam_k2': nrm(ks[23], (DEPTH, DA_QK_DIM), 0.1),
        'attn_subln': 1.0 + nrm(ks[24], (DEPTH, DA_V_DIM), 0.05),
        'p_lru': nrm(ks[25], (DEPTH, D_RNN, D_MODEL), D_RNN ** -0.5),
        'p_attn': nrm(ks[26], (DEPTH, V_COLS, D_MODEL), V_COLS ** -0.5),
        'w_out': nrm(ks[27], (DEPTH, D_MODEL, D_MODEL), D_MODEL ** -0.5),
    }


def reference(x_prompt, x_sample, c, cache_k, cache_v, state_lru, c_ctx, w_mod, b_mod, g_pre, g_post,
              ffn_w1, ffn_w2, w_in, conv_w, conv_b, lru_wa, lru_ba, lru_wx, lru_bx, lru_lambda,
              lam_q1, lam_k1, lam_q2, lam_k2, attn_subln, p_lru, p_attn, w_out):
    def layer_params(l):
        return {'g_pre': g_pre[l], 'g_post': g_post[l], 'ffn_w1': ffn_w1[l], 'ffn_w2': ffn_w2[l],
                'w_in': w_in[l], 'conv_w': conv_w[l], 'conv_b': conv_b[l], 'lru_wa': lru_wa[l],
                'lru_ba': lru_ba[l], 'lru_wx': lru_wx[l], 'lru_bx': lru_bx[l], 'lru_lambda': lru_lambda[l],
                'lam_q1': lam_q1[l], 'lam_k1': lam_k1[l], 'lam_q2': lam_q2[l], 'lam_k2': lam_k2[l],
                'attn_subln': attn_subln[l], 'p_lru': p_lru[l], 'p_attn': p_attn[l], 'w_out': w_out[l]}

    x = x_prompt
    ks_, vs_, ss_ = [], [], []
    silu_ctx = jax.nn.silu(c_ctx)
    for l in range(DEPTH):
        mod = (silu_ctx @ w_mod[l] + b_mod[l])[None, None, :]
        x, (k_l, v_l, s_l) = trunk_layer(x, mod, layer_params(l), l, None, None, None)
        ks_.append(k_l)
        vs_.append(v_l)
        ss_.append(s_l)
    y_prompt = x
    new_cache_k = jnp.stack(ks_, axis=1)
    new_cache_v = jnp.stack(vs_, axis=1)
    new_state_lru = jnp.stack(ss_, axis=1)

    x = x_sample
    silu_c = jax.nn.silu(c)
    for l in range(DEPTH):
        mod = (silu_c @ w_mod[l] + b_mod[l])[:, None, :]
        x, _ = trunk_layer(x, mod, layer_params(l), l, cache_k[:, l], cache_v[:, l], state_lru[:, l])
    y_sample = x
    return (y_prompt, y_sample, new_cache_k, new_cache_v, new_state_lru)
```

```python
import contextlib
import math
import numpy as np
import concourse.bass as bass
import concourse.mybir as mybir
from concourse.bass_utils import run_bass_kernel_spmd

F32 = mybir.dt.float32
F32R = mybir.dt.float32r
AF = mybir.ActivationFunctionType
ALU = mybir.AluOpType

D = 2048
KC = 16
T = 512
NTOK = 3072
NT = 6
DFF = 5504
FC = 43
INC = 11264
DEPTH = 2
PAST = 512
EPS = 1e-6
NCORES = 8
FF_GROUPS = [(0, 12), (12, 12), (24, 12), (36, 7)]
L_RUN = DEPTH


class Buf:
    __slots__ = ("name", "w", "r", "excl")

    def __init__(self, name, excl=False):
        self.name = name
        self.w = {}
        self.r = {}
        self.excl = excl


class DSem:
    __slots__ = ("sem", "val", "key")

    def __init__(self, sem, key):
        self.sem = sem
        self.val = 0
        self.key = key


class Sched:
    ENG = ("pe", "act", "dve", "pool", "sp")

    def __init__(self, nc, stack):
        self.nc = nc
        self.stack = stack
        self.ops = {e: [] for e in self.ENG}
        self.sem = {}
        self.cnt = {}
        self.semobj = {}
        for e in ("pe", "act", "dve", "pool"):
            s = stack.enter_context(nc.semaphore("es_" + e))
            self.sem[e] = "es_" + e
            self.semobj["es_" + e] = s
            self.cnt[e] = 0
        self.seen = {e: {} for e in self.ENG}
        self.dsems = []
        self.out_need = {}
        self.nwait = 0
        self.nops = 0

    def dsem(self):
        key = "ds%d" % len(self.dsems)
        s = self.stack.enter_context(self.nc.semaphore(key))
        self.semobj[key] = s
        d = DSem(s, key)
        self.dsems.append(d)
        return d

    def _wait_map(self, eng, need):
        seen = self.seen[eng]
        own = self.sem.get(eng)
        for k, v in need.items():
            if eng == "pe" and k == own:
                continue
            if seen.get(k, 0) >= v:
                continue
            seen[k] = v
            so = self.semobj[k]
            self.ops[eng].append(lambda e, so=so, v=v: e.wait_ge(so, v))
            self.nwait += 1

    def _waits(self, eng, reads, writes):
        need = {}
        for b in reads:
            for k, v in b.w.items():
                if need.get(k, 0) < v:
                    need[k] = v
        for b in writes:
            for d in (b.w, b.r):
                for k, v in d.items():
                    if need.get(k, 0) < v:
                        need[k] = v
        self._wait_map(eng, need)

    def op(self, eng, fn, reads=(), writes=()):
        ex = [b for b in reads if b.excl]
        if ex:
            writes = list(writes) + [b for b in ex if b not in writes]
        self._waits(eng, reads, writes)
        self.cnt[eng] += 1
        k = self.sem[eng]
        v = self.cnt[eng]
        so = self.semobj[k]
        self.ops[eng].append(lambda e, so=so: fn(e).then_inc(so, 1))
        self.nops += 1
        for b in reads:
            if b.r.get(k, 0) < v:
                b.r[k] = v
        for b in writes:
            b.w = {k: v}
            b.r = {}

    def dma(self, eng, fn, ds, reads=(), writes=(), acc=(), out=False):
        self._waits(eng, reads, writes)
        if acc:
            need = {}
            for b in acc:
                for k_, v_ in b.r.items():
                    if need.get(k_, 0) < v_:
                        need[k_] = v_
            self._wait_map(eng, need)
        ds.val += 16
        k, v, so = ds.key, ds.val, ds.sem
        self.ops[eng].append(lambda e, so=so: fn(e).then_inc(so, 16))
        self.nops += 1
        for b in reads:
            if b.r.get(k, 0) < v:
                b.r[k] = v
        for b in writes:
            b.w = {k: v}
            b.r = {}
        for b in acc:
            if b.w.get(k, 0) < v:
                b.w[k] = v
        if out:
            self.out_need[k] = v

    def barrier(self):
        need = {self.sem[e]: self.cnt[e] for e in self.cnt if self.cnt[e] > 0}
        for d in self.dsems:
            if d.val > 0:
                need[d.key] = d.val
        for e in self.ENG:
            n2 = dict(need)
            if e in self.sem:
                n2.pop(self.sem[e], None) if e == "pe" else None
            self._wait_map(e, n2)

    def emit(self):
        nc = self.nc
        with nc.Block() as block:
            @block.sync
            def _(e):
                for f in self.ops["sp"]:
                    f(e)

            @block.tensor
            def _(e):
                for f in self.ops["pe"]:
                    f(e)

            @block.scalar
            def _(e):
                for f in self.ops["act"]:
                    f(e)

            @block.vector
            def _(e):
                for f in self.ops["dve"]:
                    f(e)

            @block.gpsimd
            def _(e):
                for f in self.ops["pool"]:
                    f(e)


def build_program():
    nc = bass.Bass("TRN2", target_bir_lowering=False)

    def din(name, shape):
        return nc.dram_tensor(name, list(shape), F32, kind="ExternalInput").ap()

    def dout(name, shape):
        return nc.dram_tensor(name, list(shape), F32, kind="ExternalOutput").ap()

    import os as _os0
    _DBG = bool(_os0.environ.get("K_DBG", ""))

    def dscr(name, shape):
        if _DBG:
            return nc.dram_tensor(name, list(shape), F32, kind="ExternalOutput").ap()
        return nc.dram_tensor(name, list(shape), F32).ap()

    x_in = din("x_in", [NTOK, D])
    cvec = din("cvec", [2, D])
    cache_k = din("cache_k", [DEPTH, PAST, 8, 128])
    cache_v = din("cache_v", [DEPTH, PAST, 8, 128])
    state_in = din("state_in", [DEPTH, 2, D])
    w_mod = din("w_mod", [DEPTH, D, 9 * D])
    b_mod = din("b_mod", [DEPTH, 9 * D])
    g_pre = din("g_pre", [DEPTH, 3, D])
    g_post = din("g_post", [DEPTH, 3, D])
    ffn_w1 = din("ffn_w1", [DEPTH, 2, D, 2 * DFF])
    ffn_w2 = din("ffn_w2", [DEPTH, 2, DFF, D])
    w_in = din("w_in", [DEPTH, D, INC])
    conv_w = din("conv_w", [DEPTH, 4, D])
    conv_b = din("conv_b", [DEPTH, D])
    lru_wa = din("lru_wa", [DEPTH, 2, 16, 128, 128])
    lru_ba = din("lru_ba", [DEPTH, 2, D])
    lru_wx = din("lru_wx", [DEPTH, 2, 16, 128, 128])
    lru_bx = din("lru_bx", [DEPTH, 2, D])
    lru_lambda = din("lru_lambda", [DEPTH, 2, D])
    lam_q1 = din("lam_q1", [DEPTH, 64])
    lam_k1 = din("lam_k1", [DEPTH, 64])
    lam_q2 = din("lam_q2", [DEPTH, 64])
    lam_k2 = din("lam_k2", [DEPTH, 64])
    attn_subln = din("attn_subln", [DEPTH, 128])
    p_lru = din("p_lru", [DEPTH, D, D])
    p_attn = din("p_attn", [DEPTH, 1024, D])
    w_out = din("w_out", [DEPTH, D, D])
    c_ident = din("c_ident", [128, 128])
    c_perm = din("c_perm", [128, 128])
    c_cos = din("c_cos", [128, 2048])
    c_sin = din("c_sin", [128, 2048])

    y_out = dout("y_out", [NTOK, D])
    nck = dout("nck", [4, DEPTH, 256, 1024])
    ncv = dout("ncv", [4, DEPTH, 256, 1024])
    nst = dout("nst", [256, 128])

    xs_d = dscr("xs_d", [D, NTOK])
    win_d = dscr("win_d", [INC, NTOK])
    rec_d = dscr("rec_d", [D, NTOK])
    o_d = dscr("o_d", [1024, NTOK])

    with contextlib.ExitStack() as st:
        S = Sched(nc, st)

        def sb(name, shape, dt=F32):
            return st.enter_context(nc.sbuf_tensor(name, list(shape), dt))

        def pst(name):
            return st.enter_context(nc.psum_tensor(name, [128, 512], F32))

        ident = sb("ident", [128, 128])
        perm = sb("perm", [128, 128])
        ones = sb("ones", [128, 128], F32R)
        WB = [sb("wb%d" % i, [128, 4, 512], F32R) for i in range(3)]
        BWB = [Buf("wb%d" % i) for i in range(3)]
        DWB = [S.dsem() for _ in range(3)]
        GW = [sb("gw%d" % i, [128, 4, 128], F32R) for i in range(2)]
        BGW = [Buf("gw%d" % i) for i in range(2)]
        DGW = [S.dsem() for _ in range(2)]
        NSM = 6
        SM = [sb("sm%d" % i, [128, 512]) for i in range(NSM)]
        BSM = [Buf("sm%d" % i) for i in range(NSM)]
        DSM = [S.dsem() for _ in range(NSM)]
        SQ = [sb("sq%d" % i, [128, 512], F32R) for i in range(2)]
        BSQ = [Buf("sq%d" % i) for i in range(2)]
        rs = sb("rs", [128, 512]);  Brs = Buf("rs")
        modT = sb("modT", [128, DEPTH, 144, 2]);  BmodT = Buf("modT")
        bmodT = sb("bmodT", [128, DEPTH, 144])
        scT = sb("scT", [128, 16, 2], F32R)
        cT = sb("cT", [128, 2, 16])
        gpreT = sb("gpreT", [128, DEPTH * 3, 16])
        gpostT = sb("gpostT", [128, DEPTH * 3, 16])
        AV = sb("AV", [128, DEPTH, 2, 3, 16])
        GV = sb("GV", [128, DEPTH, 2, 3, 16])
        convT = sb("convT", [128, DEPTH * 4, 16])
        convbT = sb("convbT", [128, DEPTH, 16])
        baT = sb("baT", [128, DEPTH * 2, 16])
        bxT = sb("bxT", [128, DEPTH * 2, 16])
        lamT = sb("lamT", [128, DEPTH * 2, 16])
        cAT = sb("cAT", [128, DEPTH * 2, 16])
        h0T = sb("h0T", [128, DEPTH * 2, 16])
        lamv = sb("lamv", [128, DEPTH, 4, 64])
        lamr = sb("lamr", [128, DEPTH, 4])
        neglam = sb("neglam", [128, DEPTH])
        sublnT = sb("sublnT", [128, DEPTH])
        gsub = sb("gsub", [128, DEPTH])
        ssb = sb("ssb", [128, 256])
        Bssb = Buf("ssb")
        Bvec = Buf("vec")
        Bconst = Buf("const")

        arena = sb("arenaF", [128, 16384])
        arenaR = sb("arenaR", [128, 20480], F32R)

        PS = [pst("ps%d" % i) for i in range(8)]
        BPS = [Buf("ps%d" % i, excl=True) for i in range(8)]

        def av(off, n):
            return arena[:, off:off + n]

        def avr(off, n):
            return arenaR[:, off:off + n]

        xT = av(0, 8192).rearrange("p (c t) -> p c t", c=16)
        yT = av(8192, 8192).rearrange("p (c t) -> p c t", c=16)
        hTr = avr(0, 8192).rearrange("p (c t) -> p c t", c=16)
        actTr = avr(8192, 8192).rearrange("p (c t) -> p c t", c=16)
        oTr = avr(16384, 4096).rearrange("p (c t) -> p c t", c=8)
        BoT = Buf("oT")
        BxT = [Buf("xT%d" % i) for i in range(16)]; BhT = [Buf("hT%d" % i) for i in range(16)]
        ByT = [Buf("yT%d" % i) for i in range(16)]; Bact = [Buf("act%d" % i) for i in range(16)]
        dxT = S.dsem(); dhT = S.dsem(); dyT = S.dsem(); d_oT = S.dsem(); d_xst = S.dsem(); d_yout = S.dsem()

        state = {"wslot": 0, "ps": 0, "sm": 0, "sq": 0, "eb": 0, "bs": 0}
        _PB = {}

        def PBuf(name):
            if name not in _PB:
                _PB[name] = Buf(name)
            return _PB[name]

        PD = [S.dsem() for _ in range(9)]

        def act(out, in_, func, reads, writes, bias=None, scale=None):
            kw = {}
            if bias is not None:
                kw["bias"] = bias
            if scale is not None:
                kw["scale"] = scale
            S.op("act", lambda e: e.activation(out, in_, func, **kw), reads, writes)

        def tt(out, in0, in1, op, reads, writes):
            S.op("dve", lambda e: e.tensor_tensor(out, in0, in1, op), reads, writes)

        def ts(out, in0, s1, s2, op0, op1, reads, writes):
            if op1 is None:
                S.op("dve", lambda e: e.tensor_scalar(out, in0, s1, None, op0), reads, writes)
            else:
                S.op("dve", lambda e: e.tensor_scalar(out, in0, s1, s2, op0, op1), reads, writes)

        def stt(out, in0, scalar, in1, op0, op1, reads, writes):
            S.op("dve", lambda e: e.scalar_tensor_tensor(out, in0, scalar, in1, op0, op1), reads, writes)

        def vcopy(out, in_, reads, writes):
            S.op("dve", lambda e: e.tensor_copy(out, in_), reads, writes)

        def acopy(out, in_, reads, writes):
            act(out, in_, AF.Copy, reads, writes)

        def recip(out, in_, reads, writes):
            S.op("dve", lambda e: e.reciprocal(out, in_), reads, writes)

        def mm(out, lhsT, rhs, start, stop, reads, writes):
            S.op("pe", lambda e: e.matmul(out, lhsT, rhs, start=start, stop=stop), reads, writes)

        def tr(out, in_, reads, writes):
            S.op("pe", lambda e: e.transpose(out, in_, ident[:]), list(reads) + [Bconst], writes)

        def dma(q, out, in_, ds, reads, writes, slow=False, acc=(), final=False):
            if slow:
                S.dma(q, lambda e: e.dma_start(out=out, in_=in_, allow_slow_non_contiguous=True), ds, reads, writes, acc, final)
            else:
                S.dma(q, lambda e: e.dma_start(out=out, in_=in_), ds, reads, writes, acc, final)

        def next_ps(lo=0, hi=4):
            i = lo + state["ps"] % (hi - lo)
            state["ps"] += 1
            return i

        def next_sm():
            i = state["sm"] % NSM
            state["sm"] += 1
            return i

        BANKSETS = ([0, 1, 2, 3], [4, 5, 6, 7])
        def _wview(off):
            return avr(off, 2048).rearrange("p (k c) -> p k c", k=4)
        WPOOL_ALL = [(WB[i][:], BWB[i], DWB[i], None) for i in range(3)]
        WPOOL_ALL += [(_wview(16384 + i * 2048), Buf("wbo%d" % i), S.dsem(), "o") for i in range(2)]
        WPOOL_ALL += [(_wview(8192 + i * 2048), Buf("wba%d" % i), S.dsem(), "a") for i in range(4)]
        BWBX_O = [WPOOL_ALL[3][1], WPOOL_ALL[4][1]]
        BWBX_A = [WPOOL_ALL[5 + i][1] for i in range(4)]
        WPOOLS = {"c": [0, 1, 2], "ffn": [0, 1, 2, 3, 4, 8], "win": [0, 1, 2, 3, 4, 5, 6, 7, 8]}
        state["wpool"] = "c"

        def inherit(dst, srcs):
            for b_ in srcs:
                for dd in (b_.w, b_.r):
                    for k_, v_ in dd.items():
                        if dst.r.get(k_, 0) < v_:
                            dst.r[k_] = v_

        def set_wpool(name):
            if name == "ffn":
                for c_ in range(12):
                    inherit(Bact[c_], [BWBX_A[c_ // 4]])
                inherit(BWBX_A[3], Bact[12:16])
                for b_ in BWBX_O:
                    inherit(b_, [BoT])
            elif name == "win":
                for i_, b_ in enumerate(BWBX_A):
                    inherit(b_, Bact[4 * i_:4 * i_ + 4])
                for b_ in BWBX_O:
                    inherit(b_, [BoT])
            state["wpool"] = name

        def linear_blocked(Wv, kc0, nk, col0, nch, rhs_fn, rhs_bufs, epilogue, TT=T, bankset=None):
            if bankset is None:
                bankset = BANKSETS[state["bs"] % 2]
                state["bs"] += 1
            cw = nch * 128
            S._waits("pe", [], [BPS[b_] for b_ in bankset[:nch]])
            for kb0 in range(0, nk, 4):
                kn = min(4, nk - kb0)
                pool_ = WPOOLS[state["wpool"]]
                wt_, bw_, dw_, reg_ = WPOOL_ALL[pool_[state["wslot"] % len(pool_)]]
                state["wslot"] += 1
                dma("pool", wt_[:, 0:kn, 0:cw], Wv[:, kc0 + kb0:kc0 + kb0 + kn, col0:col0 + cw], dw_, (), [bw_])
                for i in range(kn):
                    kk = kb0 + i
                    for j in range(nch):
                        pb = bankset[j]
                        mm(PS[pb][:, 0:TT], wt_[:, i, j * 128:(j + 1) * 128], rhs_fn(kk), kk == 0, kk == nk - 1,
                           [bw_] + list(rhs_bufs(kk) if callable(rhs_bufs) else rhs_bufs), [BPS[pb]])
            for j in range(nch):
                epilogue(j, bankset[j])
            return bankset

        dset = S.dsem()
        dset_c = S.dsem()
        dma("sp", ident[:], c_ident, dset_c, (), (), acc=[Bconst])
        dma("sp", perm[:], c_perm, dset_c, (), (), acc=[Bconst])
        onesf = sb("onesf", [128, 128])
        S.op("dve", lambda e: e.memset(onesf[:], 1.0), (), [Bconst])
        S.op("dve", lambda e: e.tensor_copy(ones[:], onesf[:]), [Bconst], [Bconst])

        def vload(dst, src1d):
            dma("sp", dst, src1d.rearrange("(c p) -> p c", p=128), dset, (), (), slow=True, acc=[Bvec])

        for l in range(DEPTH):
            vload(bmodT[:, l, :], b_mod[l])
            vload(convbT[:, l, :], conv_b[l])
            for i in range(3):
                vload(gpreT[:, l * 3 + i, :], g_pre[l, i])
                vload(gpostT[:, l * 3 + i, :], g_post[l, i])
            for j in range(4):
                vload(convT[:, l * 4 + j, :], conv_w[l, j])
            for d_ in range(2):
                vload(baT[:, l * 2 + d_, :], lru_ba[l, d_])
                vload(bxT[:, l * 2 + d_, :], lru_bx[l, d_])
                vload(lamT[:, l * 2 + d_, :], lru_lambda[l, d_])
                vload(h0T[:, l * 2 + d_, :], state_in[l, d_])
            for i, lv in enumerate((lam_q1, lam_k1, lam_q2, lam_k2)):
                dma("sp", lamv[:, l, i, :], lv[l:l + 1, :].broadcast_to([128, 64]), dset, (), (), acc=[Bvec])
            dma("sp", sublnT[:, l:l + 1], attn_subln[l].rearrange("(p o) -> p o", o=1), dset, (), (), slow=True, acc=[Bvec])
        for g in range(2):
            vload(cT[:, g, :], cvec[g])
        for g in range(2):
            act(scT[:, :, g], cT[:, g, :], AF.Silu, [Bvec], [Bvec])
        act(cAT[:], lamT[:], AF.Exp, [Bvec], [Bvec], scale=-1.0)
        act(cAT[:], cAT[:], AF.Ln, [Bvec], [Bvec], bias=1.0)
        ts(cAT[:], cAT[:], -8.0, None, ALU.mult, None, [Bvec], [Bvec])
        for l in range(DEPTH):
            lam_init = 0.8 - 0.6 * math.exp(-0.3 * l)
            for i in range(2):
                tt(lamv[:, l, 2 * i, :], lamv[:, l, 2 * i, :], lamv[:, l, 2 * i + 1, :], ALU.mult, [Bvec], [Bvec])
                S.op("dve", lambda e, l=l, i=i: e.reduce_sum(lamr[:, l, i:i + 1], lamv[:, l, 2 * i, :], axis=mybir.AxisListType.X), [Bvec], [Bvec])
            act(lamr[:, l, 0:2], lamr[:, l, 0:2], AF.Exp, [Bvec], [Bvec])
            stt(neglam[:, l:l + 1], lamr[:, l, 1:2], -lam_init, lamr[:, l, 0:1], ALU.add, ALU.subtract, [Bvec], [Bvec])
            ts(gsub[:, l:l + 1], sublnT[:, l:l + 1], 1.0 - lam_init, None, ALU.mult, None, [Bvec], [Bvec])
        def mod_groups(l):
            wv = w_mod[l].rearrange("(kc p) n -> p kc n", p=128)
            for j4 in range(36):
                def ep_mod(j, pb, j4=j4, l=l):
                    jj = j4 * 4 + j
                    ts(modT[:, l, jj, :], PS[pb][:, 0:2], bmodT[:, l, jj:jj + 1], None, ALU.add, None, [BPS[pb], Bvec], [BmodT])
                linear_blocked(wv, 0, 16, j4 * 512, 4, lambda kk: scT[:, kk, :], [Bvec], ep_mod, TT=2)
                yield
            for g in range(2):
                for i in range(3):
                    coef = 1.0 if i == 1 else 0.5
                    sc = modT[:, l, (3 * i + 1) * 16:(3 * i + 2) * 16, g]
                    gt = modT[:, l, (3 * i + 2) * 16:(3 * i + 3) * 16, g]
                    stt(AV[:, l, g, i, :], sc, 1.0, gpreT[:, l * 3 + i, :], ALU.add, ALU.mult, [BmodT, Bvec], [Bvec])
                    stt(GV[:, l, g, i, :], gt, coef, gpostT[:, l * 3 + i, :], ALU.mult, ALU.mult, [BmodT, Bvec], [Bvec])

        for _ in mod_groups(0):
            pass
        mod_gen = {"g": mod_groups(1) if L_RUN > 1 else None}

        def pump_mod(k):
            g_ = mod_gen["g"]
            if g_ is None:
                return
            for _ in range(k):
                try:
                    next(g_)
                except StopIteration:
                    mod_gen["g"] = None
                    return

        def SH(l, g, i, c):
            return modT[:, l, (3 * i) * 16 + c:(3 * i) * 16 + c + 1, g]

        def rstd_from(src, Bsrc, nch, TT, inv_n):
            for c in range(nch):
                q = state["sq"] % 2
                state["sq"] += 1
                act(SQ[q][:, 0:TT], src(c), AF.Square, [Bsrc[c]], [BSQ[q]])
                mm(PS[4][:, 0:TT], ones[:], SQ[q][:, 0:TT], c == 0, c == nch - 1, [BSQ[q], Bconst], [BPS[4]])
            act(rs[:, 0:TT], PS[4][:, 0:TT], AF.Sqrt, [BPS[4]], [Brs], bias=epsb[:, 0:1], scale=inv_n)
            recip(rs[:, 0:TT], rs[:, 0:TT], [Brs], [Brs])

        epsb = sb("epsb", [128, 1])
        S.op("dve", lambda e: e.memset(epsb[:], EPS), (), [Bconst])

        def normmod(l, g, i):
            rstd_from(lambda c: xT[:, c, :], BxT, 16, T, 1.0 / D)
            for c in range(16):
                k = next_sm()
                stt(SM[k][:], xT[:, c, :], AV[:, l, g, i, c:c + 1], rs[:], ALU.mult, ALU.mult, [BxT[c], Bvec, Brs], [BSM[k]])
                act(hTr[:, c, :], SM[k][:], AF.Identity, [BSM[k], BmodT], [BhT[c]], bias=SH(l, g, i, c))

        def postnorm(l, g, i):
            rstd_from(lambda c: yT[:, c, :], ByT, 16, T, 1.0 / D)
            for c in range(16):
                k = next_sm()
                stt(SM[k][:], yT[:, c, :], GV[:, l, g, i, c:c + 1], rs[:], ALU.mult, ALU.mult, [ByT[c], Bvec, Brs], [BSM[k]])
                tt(xT[:, c, :], xT[:, c, :], SM[k][:], ALU.add, [BSM[k], BxT[c]], [BxT[c]])

        def ffn(l, g, which):
            i = 0 if which == 0 else 2
            normmod(l, g, i)
            set_wpool("ffn")
            w1v = ffn_w1[l, which].rearrange("(kc p) n -> p kc n", p=128)
            w2v = ffn_w2[l, which].rearrange("(kc p) n -> p kc n", p=128)
            actF = actTr.bitcast(F32)
            for gi, (f0, fl) in enumerate(FF_GROUPS):
                for c0 in range(0, fl, 4):
                    nch = min(4, fl - c0)

                    def ep_g(j, pb, c0=c0):
                        act(actTr[:, c0 + j, :], PS[pb][:], AF.Silu, [BPS[pb]], [Bact[c0 + j]])

                    def ep_u(j, pb, c0=c0):
                        tt(actTr[:, c0 + j, :], actF[:, c0 + j, :], PS[pb][:], ALU.mult, [Bact[c0 + j], BPS[pb]], [Bact[c0 + j]])

                    linear_blocked(w1v, 0, 16, (f0 + c0) * 128, nch, lambda kk: hTr[:, kk, :], lambda kk: [BhT[kk]], ep_g, bankset=BANKSETS[0])
                    linear_blocked(w1v, 0, 16, DFF + (f0 + c0) * 128, nch, lambda kk: hTr[:, kk, :], lambda kk: [BhT[kk]], ep_u, bankset=BANKSETS[1])
                for n4 in range(4):
                    def ep_y(j, pb, n4=n4, gi=gi):
                        n = n4 * 4 + j
                        if gi == 0:
                            acopy(yT[:, n, :], PS[pb][:], [BPS[pb]], [ByT[n]])
                        else:
                            tt(yT[:, n, :], PS[pb][:], yT[:, n, :], ALU.add, [BPS[pb], ByT[n]], [ByT[n]])
                    linear_blocked(w2v, f0, fl, n4 * 512, 4, lambda kk: actTr[:, kk, :], lambda kk: [Bact[kk]], ep_y)
            state["wpool"] = "c"
            postnorm(l, g, i)

        def load_x_tile(ti):
            t0 = ti * T
            yv = av(8192, 8192).rearrange("p (b f) -> p b f", b=4)
            dma("sp", yv, x_in[t0:t0 + T, :].rearrange("(b p) f -> p b f", p=128), dyT, (), ByT)
            for c in range(16):
                pb = 5 if c % 2 == 0 else 6
                for b in range(4):
                    tr(PS[pb][:, b * 128:(b + 1) * 128], yv[:, b, c * 128:(c + 1) * 128], ByT[4 * b:4 * b + 4], [BPS[pb]])
                if c % 2 == 0:
                    vcopy(xT[:, c, :], PS[pb][:], [BPS[pb]], [BxT[c]])
                else:
                    acopy(xT[:, c, :], PS[pb][:], [BPS[pb]], [BxT[c]])

        def store_y_tile(ti):
            t0 = ti * T
            yv = av(8192, 8192).rearrange("p (b f) -> p b f", b=4)
            for b in range(4):
                for c4 in range(4):
                    pb = 5 if (b * 4 + c4) % 2 == 0 else 6
                    for cc in range(4):
                        c = c4 * 4 + cc
                        tr(PS[pb][:, cc * 128:(cc + 1) * 128], xT[:, c, b * 128:(b + 1) * 128], [BxT[c]], [BPS[pb]])
                    if c4 % 2 == 0:
                        vcopy(yv[:, b, c4 * 512:(c4 + 1) * 512], PS[pb][:], [BPS[pb]], [ByT[4 * b + c4]])
                    else:
                        acopy(yv[:, b, c4 * 512:(c4 + 1) * 512], PS[pb][:], [BPS[pb]], [ByT[4 * b + c4]])
            dma("sp", y_out[t0:t0 + T, :].rearrange("(b p) f -> p b f", p=128), yv, d_yout, ByT, (), final=True)

        Bout = Buf("out")
        Bwin = [Buf("win%d" % i) for i in range(NT)]
        Bxs = [Buf("xs%d" % i) for i in range(NT)]
        Brec = [Buf("rec%d" % i) for i in range(NT)]
        Bo = [Buf("o%d" % i) for i in range(NT)]

        def phase_A(l, ti, g):
            t0 = ti * T
            ffn(l, g, 0)
            normmod(l, g, 1)
            set_wpool("win")
            wv = w_in[l].rearrange("(kc p) n -> p kc n", p=128)
            for n4 in range(INC // 512):
                def ep_w(j, pb, n4=n4):
                    n = n4 * 4 + j
                    k = next_sm()
                    if n % 2 == 0:
                        vcopy(SM[k][:], PS[pb][:], [BPS[pb]], [BSM[k]])
                    else:
                        acopy(SM[k][:], PS[pb][:], [BPS[pb]], [BSM[k]])
                    dma("sp", win_d[n * 128:(n + 1) * 128, t0:t0 + T], SM[k][:], DSM[k], [BSM[k]], (), acc=[Bwin[ti]])
                linear_blocked(wv, 0, 16, n4 * 512, 4, lambda kk: hTr[:, kk, :], lambda kk: [BhT[kk]], ep_w)
            state["wpool"] = "c"
            dma("sp", xs_d[:, t0:t0 + T].rearrange("(c p) t -> p c t", p=128), xT, d_xst, BxT, [Bxs[ti]])

        def phase_C(l, ti, g):
            t0 = ti * T
            dma("sp", xT, xs_d[:, t0:t0 + T].rearrange("(c p) t -> p c t", p=128), dxT, [Bxs[ti]], BxT)
            dma("pool", actTr, rec_d[:, t0:t0 + T].rearrange("(c p) t -> p c t", p=128), dhT, [Brec[ti]], Bact + BWBX_A)
            dma("pool", oTr, o_d[:, t0:t0 + T].rearrange("(c p) t -> p c t", p=128), d_oT, [Bo[ti]], [BoT] + BWBX_O)
            plv = p_lru[l].rearrange("(kc p) n -> p kc n", p=128)
            pav = p_attn[l].rearrange("(kc p) n -> p kc n", p=128)
            wov = w_out[l].rearrange("(kc p) n -> p kc n", p=128)
            for n4 in range(4):
                def ep_none(j, pb):
                    pass

                def ep_merge(j, p2, n4=n4):
                    n = n4 * 4 + j
                    p1 = BANKSETS[0][j]
                    k1 = next_sm(); k2 = next_sm()
                    dma("sp", SM[k1][:], win_d[7168 + n * 128:7168 + (n + 1) * 128, t0:t0 + T], DSM[k1], [Bwin[ti]], [BSM[k1]])
                    dma("sp", SM[k2][:], win_d[7168 + 2048 + n * 128:7168 + 2048 + (n + 1) * 128, t0:t0 + T], DSM[k2], [Bwin[ti]], [BSM[k2]])
                    act(SM[k1][:], SM[k1][:], AF.Sigmoid, [BSM[k1]], [BSM[k1]])
                    act(SM[k2][:], SM[k2][:], AF.Sigmoid, [BSM[k2]], [BSM[k2]])
                    tt(SM[k1][:], SM[k1][:], PS[p1][:], ALU.mult, [BSM[k1], BPS[p1]], [BSM[k1]])
                    tt(SM[k2][:], SM[k2][:], PS[p2][:], ALU.mult, [BSM[k2], BPS[p2]], [BSM[k2]])
                    tt(hTr[:, n, :], SM[k1][:], SM[k2][:], ALU.add, [BSM[k1], BSM[k2]], [BhT[n]])
                linear_blocked(plv, 0, 16, n4 * 512, 4, lambda kk: actTr[:, kk, :], lambda kk: [Bact[kk]], ep_none, bankset=BANKSETS[0])
                linear_blocked(pav, 0, 8, n4 * 512, 4, lambda kk: oTr[:, kk, :], [BoT], ep_merge, bankset=BANKSETS[1])
            for n4 in range(4):
                def ep_o(j, pb, n4=n4):
                    n = n4 * 4 + j
                    if n % 2 == 0:
                        vcopy(yT[:, n, :], PS[pb][:], [BPS[pb]], [ByT[n]])
                    else:
                        acopy(yT[:, n, :], PS[pb][:], [BPS[pb]], [ByT[n]])
                linear_blocked(wov, 0, 16, n4 * 512, 4, lambda kk: hTr[:, kk, :], lambda kk: [BhT[kk]], ep_o)
            postnorm(l, g, 1)
            ffn(l, g, 1)

        def phase_B_lru(l, t0, SS, nseg, latent):
            L = SS // nseg
            sl_ = [av(i * 2048, SS) for i in range(8)]
            Bs = [PBuf("lru%d" % i) for i in range(8)]
            xr, Bxr = sl_[0], Bs[0]
            gr, Bgr = sl_[0], Bs[0]
            yf, Byf = sl_[1], Bs[1]
            bA = [sl_[2], sl_[4]]; BbA = [Bs[2], Bs[4]]
            bB = [sl_[3], sl_[5]]; BbB = [Bs[3], Bs[5]]
            tm = [sl_[6], sl_[7]]; Btm = [Bs[6], Bs[7]]
            yr = avr(0, SS); Byr = PBuf("yr")
            dl = PD
            tiles = range(t0 // T, (t0 + SS) // T)
            rd = [Bwin[i] for i in tiles]

            def seg(ap, a, b):
                return ap.rearrange("p (s t) -> p s t", s=nseg)[:, :, a:b]

            def ptt(out, in0, in1, op, reads, writes):
                if mod_gen["g"] is not None:
                    tt(out, in0, in1, op, reads, writes)
                else:
                    S.op("pool", lambda e: e.tensor_tensor(out, in0, in1, op), reads, writes)

            for n in range(16):
                dma("sp", xr, win_d[n * 128:(n + 1) * 128, t0:t0 + SS], dl[0], rd, [Bxr])
                gs = n % 2
                for d_ in range(2):
                    dma("pool", GW[gs][:, d_ * 2, :], lru_wa[l, d_, n], DGW[gs], (), (), acc=[BGW[gs]])
                    dma("pool", GW[gs][:, d_ * 2 + 1, :], lru_wx[l, d_, n], DGW[gs], (), (), acc=[BGW[gs]])
                cw = lambda j: convT[:, l * 4 + j, n:n + 1]
                act(yf, xr, AF.Identity, [Bxr, Bvec], [Byf], bias=convbT[:, l, n:n + 1], scale=cw(2))
                stt(seg(yf, 2, L), seg(xr, 0, L - 2), cw(0), seg(yf, 2, L), ALU.mult, ALU.add, [Bxr, Bvec, Byf], [Byf])
                stt(seg(yf, 1, L), seg(xr, 0, L - 1), cw(1), seg(yf, 1, L), ALU.mult, ALU.add, [Bxr, Bvec, Byf], [Byf])
                stt(seg(yf, 0, L - 1), seg(xr, 1, L), cw(3), seg(yf, 0, L - 1), ALU.mult, ALU.add, [Bxr, Bvec, Byf], [Byf])
                acopy(yr, yf, [Byf], [Byr])
                dma("sp", gr, win_d[2048 + n * 128:2048 + (n + 1) * 128, t0:t0 + SS], dl[1], rd, [Bgr])
                for d_ in range(2):
                    for blk in range(SS // 512):
                        sl = slice(blk * 512, (blk + 1) * 512)
                        pa = next_ps(); px = next_ps()
                        mm(PS[pa][:], GW[gs][:, d_ * 2, :], yr[:, sl], True, True, [BGW[gs], Byr], [BPS[pa]])
                        mm(PS[px][:], GW[gs][:, d_ * 2 + 1, :], yr[:, sl], True, True, [BGW[gs], Byr], [BPS[px]])
                        act(bA[d_][:, sl], PS[pa][:], AF.Sigmoid, [BPS[pa], Bvec], [BbA[d_]], bias=baT[:, l * 2 + d_, n:n + 1])
                        act(bB[d_][:, sl], PS[px][:], AF.Sigmoid, [BPS[px], Bvec], [BbB[d_]], bias=bxT[:, l * 2 + d_, n:n + 1])
                for d_ in range(2):
                    act(bA[d_], bA[d_], AF.Exp, [BbA[d_], Bvec], [BbA[d_]], scale=cAT[:, l * 2 + d_, n:n + 1])
                for d_ in range(2):
                    act(tm[d_], bA[d_], AF.Square, [BbA[d_]], [Btm[d_]])
                for d_ in range(2):
                    act(tm[d_], tm[d_], AF.Sqrt, [Btm[d_], Bconst], [Btm[d_]], bias=oneb[:, 0:1], scale=-1.0)
                for d_ in range(2):
                    ptt(bB[d_], bB[d_], tm[d_], ALU.mult, [BbB[d_], Btm[d_]], [BbB[d_]])
                for d_ in range(2):
                    tt(bB[d_], bB[d_], yf, ALU.mult, [BbB[d_], Byf], [BbB[d_]])
                for d_ in range(2):
                    for s_ in range(nseg):
                        a0, a1 = s_ * L, (s_ + 1) * L
                        init = h0T[:, l * 2 + d_, n:n + 1] if latent else 0.0
                        if d_ == 0:
                            S.op("dve", lambda e, a0=a0, a1=a1, init=init: e.tensor_tensor_scan(tm[0][:, a0:a1], bA[0][:, a0:a1], bB[0][:, a0:a1], init, ALU.mult, ALU.add),
                                 [BbA[0], BbB[0], Bvec], [Btm[0]])
                        else:
                            def rv(ap, a0=a0, a1=a1):
                                return ap[:, a0:a1][:, ::-1]
                            S.op("dve", lambda e, rv=rv, init=init: e.tensor_tensor_scan(rv(tm[1]), rv(bA[1]), rv(bB[1]), init, ALU.mult, ALU.add),
                                 [BbA[1], BbB[1], Bvec], [Btm[1]])
                    if not latent:
                        hv = tm[d_].rearrange("p (s t) -> p s t", s=nseg)
                        src = hv[:, :, L - 1] if d_ == 0 else hv[:, :, 0]
                        dstv = ssb.rearrange("p (s r) -> p s r", s=4)[:, :, l * 32 + d_ * 16 + n]
                        vcopy(dstv, src, [Btm[d_]], [Bssb])
                ptt(tm[0], tm[0], tm[1], ALU.add, [Btm[0], Btm[1]], [Btm[0]])
                act(bA[0], gr, AF.Square, [Bgr], [BbA[0]])
                ts(bA[0], bA[0], 0.044715, 1.0, ALU.mult, ALU.add, [BbA[0]], [BbA[0]])
                tt(bA[0], bA[0], gr, ALU.mult, [BbA[0], Bgr], [BbA[0]])
                act(bA[0], bA[0], AF.Sigmoid, [BbA[0]], [BbA[0]], scale=1.5957691216057308)
                ptt(tm[0], tm[0], gr, ALU.mult, [Btm[0], Bgr], [Btm[0]])
                tt(tm[1], tm[0], bA[0], ALU.mult, [Btm[0], BbA[0]], [Btm[1]])
                dma("sp", rec_d[n * 128:(n + 1) * 128, t0:t0 + SS], tm[1], dl[2], [Btm[1]], (), acc=[Brec[i] for i in tiles])
                pump_mod(3)

        oneb = sb("oneb", [128, 1])
        S.op("dve", lambda e: e.memset(oneb[:], 1.0), (), [Bconst])

        def attn_pipeline(l, seqs, latent):
            SS = seqs[0][1]
            nkb_c = 4 if latent else 0
            QT = 512 if latent else 256
            nkb = (nkb_c + SS // 128) if latent else (QT // 128)
            cosT = av(0, 2048); sinT = av(2048, 2048)
            kraw = [av(4096, 2048), av(6144, 2048)]
            vraw = [av(8192, 2048), av(10240, 2048)]
            kcs = av(12288, 512).rearrange("p (b d) -> p b d", d=128)
            vcs = av(12800, 512).rearrange("p (b d) -> p b d", d=128)
            qraw = [av(13312, 512), av(13824, 512)]
            om = [[av(14336 + (2 * t_ + m_) * 512, 512) for m_ in range(2)] for t_ in range(2)]
            KT = [avr(0, 2560), avr(11264, 2560)]
            VA = [avr(2560, 2560).rearrange("p (b d) -> p b d", d=128), avr(13824, 2560).rearrange("p (b d) -> p b d", d=128)]
            qz = [[avr(5120 + (2 * t_ + m_) * 512, 512) for m_ in range(2)] for t_ in range(2)]
            NEB = 6
            EB = [avr(8192 + i * 512, 512) for i in range(NEB)]
            Bcs = PBuf("cs")
            BKT = [PBuf("KT0"), PBuf("KT1")]; BVA = [PBuf("VA0"), PBuf("VA1")]
            Bkraw = [PBuf("kraw0"), PBuf("kraw1")]; Bvraw = [PBuf("vraw0"), PBuf("vraw1")]
            Bkcs = PBuf("kcs"); Bvcs = PBuf("vcs")
            Bqraw = [PBuf("qraw0"), PBuf("qraw1")]
            Bqz = [[PBuf("qz%d%d" % (t_, m_)) for m_ in range(2)] for t_ in range(2)]
            Bom = [[PBuf("om%d%d" % (t_, m_)) for m_ in range(2)] for t_ in range(2)]
            BEB = [PBuf("eb%d" % i) for i in range(NEB)]
            dl = PD
            kz = next_sm()
            S.op("dve", lambda e: e.memset(SM[kz][:], 0.0), (), [BSM[kz]])
            for t_ in range(2):
                vcopy(qz[t_][0][64:128, :], SM[kz][64:128, :], [BSM[kz]], [Bqz[t_][0]])
                vcopy(qz[t_][1][0:64, :], SM[kz][0:64, :], [BSM[kz]], [Bqz[t_][1]])
            if latent:
                dma("sp", cosT, c_cos, dl[8], (), (), acc=[Bcs])
                dma("sp", sinT, c_sin, dl[8], (), (), acc=[Bcs])

            items = [(si, h) for si in range(len(seqs)) for h in range(8)]
            tiles_ = [(ii, qt) for ii in range(len(items)) for qt in range(SS // QT)]

            def seq_info(ii):
                t0, _, seq_idx = seqs[items[ii][0]]
                wt = sorted(set(range(t0 // T, (t0 + SS - 1) // T + 1)))
                return t0, seq_idx, items[ii][1], wt

            def rope_to(dsts, src, Bsrc, cols, width):
                pb = next_ps()
                mm(PS[pb][:, 0:width], perm[:], src, True, True, [Bsrc, Bconst], [BPS[pb]])
                k1 = next_sm(); k2 = next_sm()
                tt(SM[k1][:, 0:width], src, cosT[:, cols], ALU.mult, [Bsrc, Bcs], [BSM[k1]])
                tt(SM[k2][:, 0:width], PS[pb][:, 0:width], sinT[:, cols], ALU.mult, [BPS[pb], Bcs], [BSM[k2]])
                for dst, ps_, bd in dsts:
                    tt(dst, SM[k1][ps_, 0:width], SM[k2][ps_, 0:width], ALU.add, [BSM[k1], BSM[k2]], [bd])

            def head_prep(ii):
                t0, seq_idx, h, wt = seq_info(ii)
                hp = ii % 2
                rd = [Bwin[i] for i in wt]
                if latent:
                    dma("sp", kcs, cache_k[l, :, h, :].rearrange("(b p) d -> p b d", p=128), dl[1], (), [Bkcs])
                    pb = next_ps()
                    for b in range(4):
                        tr(PS[pb][:, b * 128:(b + 1) * 128], kcs[:, b, :], [Bkcs], [BPS[pb]])
                    vcopy(KT[hp][:, 0:512], PS[pb][:], [BPS[pb]], [BKT[hp]])
                    dma("sp", vcs, cache_v[l, :, h, :].rearrange("(b p) d -> p b d", p=128), dl[2], (), [Bvcs])
                    acopy(VA[hp][:, 0:4, :], vcs, [Bvcs], [BVA[hp]])
                dma("sp", kraw[hp][:, 0:SS], win_d[5120 + h * 128:5120 + (h + 1) * 128, t0:t0 + SS], dl[3 + hp], rd, [Bkraw[hp]])
                dma("sp", vraw[hp][:, 0:SS], win_d[6144 + h * 128:6144 + (h + 1) * 128, t0:t0 + SS], dl[5 + hp], rd, [Bvraw[hp]])
                if latent:
                    for blk in range(SS // 512):
                        sl = slice(blk * 512, (blk + 1) * 512)
                        rope_to([(KT[hp][:, 512 + blk * 512:512 + (blk + 1) * 512], slice(0, 128), BKT[hp])], kraw[hp][:, sl], Bkraw[hp], sl, 512)
                else:
                    acopy(KT[hp][:, 0:SS], kraw[hp][:, 0:SS], [Bkraw[hp]], [BKT[hp]])
                for b4 in range((SS + 511) // 512):
                    nb = min(4, SS // 128 - b4 * 4)
                    pb = next_ps()
                    for b in range(nb):
                        tb = b4 * 4 + b
                        tr(PS[pb][:, b * 128:(b + 1) * 128], vraw[hp][:, tb * 128:(tb + 1) * 128], [Bvraw[hp]], [BPS[pb]])
                    vcopy(VA[hp][:, nkb_c + b4 * 4:nkb_c + b4 * 4 + nb, :], PS[pb][:, 0:nb * 128].rearrange("p (b d) -> p b d", d=128), [BPS[pb]], [BVA[hp]])
                    if not latent:
                        k = next_sm()
                        acopy(SM[k][:, 0:nb * 128], PS[pb][:, 0:nb * 128], [BPS[pb]], [BSM[k]])
                        s0 = seq_idx + b4 * 2
                        for si_ in range(nb // 2):
                            dma("sp", ncv[s0 + si_, l, :, h * 128:(h + 1) * 128].rearrange("(b p) d -> p b d", p=128),
                                SM[k][:, si_ * 256:(si_ + 1) * 256].rearrange("p (b d) -> p b d", d=128), DSM[k], [BSM[k]], (), final=True)
                if not latent:
                    for b4 in range((SS + 511) // 512):
                        nb = min(4, SS // 128 - b4 * 4)
                        pb = next_ps()
                        for b in range(nb):
                            tb = b4 * 4 + b
                            tr(PS[pb][:, b * 128:(b + 1) * 128], kraw[hp][:, tb * 128:(tb + 1) * 128], [Bkraw[hp]], [BPS[pb]])
                        k = next_sm()
                        vcopy(SM[k][:, 0:nb * 128], PS[pb][:, 0:nb * 128], [BPS[pb]], [BSM[k]])
                        s0 = seq_idx + b4 * 2
                        for si_ in range(nb // 2):
                            dma("sp", nck[s0 + si_, l, :, h * 128:(h + 1) * 128].rearrange("(b p) d -> p b d", p=128),
                                SM[k][:, si_ * 256:(si_ + 1) * 256].rearrange("p (b d) -> p b d", d=128), DSM[k], [BSM[k]], (), final=True)

            def q_prep(n):
                ii, qt = tiles_[n]
                t0, seq_idx, h, wt = seq_info(ii)
                tp = n % 2
                q0 = qt * QT
                rd = [Bwin[i] for i in wt]
                dma("sp", qraw[tp][:, 0:QT], win_d[4096 + h * 128:4096 + (h + 1) * 128, t0 + q0:t0 + q0 + QT], dl[7] if tp else dl[0], rd, [Bqraw[tp]])
                if latent:
                    rope_to([(qz[tp][0][0:64, 0:QT], slice(0, 64), Bqz[tp][0]), (qz[tp][1][64:128, 0:QT], slice(64, 128), Bqz[tp][1])],
                            qraw[tp][:, 0:QT], Bqraw[tp], slice(q0, q0 + QT), QT)
                else:
                    acopy(qz[tp][0][0:64, 0:QT], qraw[tp][0:64, 0:QT], [Bqraw[tp]], [Bqz[tp][0]])
                    acopy(qz[tp][1][64:128, 0:QT], qraw[tp][64:128, 0:QT], [Bqraw[tp]], [Bqz[tp][1]])

            OB = {0: (6, 7), 1: (4, 5)}
            LOOK = 3
            MID = 8

            def inner(n, hook):
                ii, qt = tiles_[n]
                hp = ii % 2; tp = n % 2
                ebuf = {}
                kb0 = 0 if latent else qt * (QT // 128)
                steps = [(kb0 + kb, m) for kb in range(nkb) for m in range(2)]

                def emit_score(i):
                    kb, m = steps[i]
                    pb = next_ps()
                    mm(PS[pb][:, 0:QT], KT[hp][:, kb * 128:(kb + 1) * 128], qz[tp][m][:, 0:QT], True, True,
                       [BKT[hp], Bqz[tp][m]], [BPS[pb]])
                    ei = state["eb"] % NEB
                    state["eb"] += 1
                    act(EB[ei][:, 0:QT], PS[pb][:, 0:QT], AF.Exp, [BPS[pb]], [BEB[ei]], scale=0.125)
                    ebuf[i] = ei

                def emit_pv(i):
                    kb, m = steps[i]
                    ei = ebuf[i]
                    po, pz = OB[m]
                    mm(PS[po][:, 0:QT], VA[hp][:, kb, :], EB[ei][:, 0:QT], kb == kb0, kb == kb0 + nkb - 1, [BVA[hp], BEB[ei]], [BPS[po]])
                    mm(PS[pz][:, 0:QT], ones[:], EB[ei][:, 0:QT], kb == kb0, kb == kb0 + nkb - 1, [Bconst, BEB[ei]], [BPS[pz]])

                for i in range(min(LOOK, len(steps))):
                    emit_score(i)
                done_hook = False
                for i in range(len(steps)):
                    if i + LOOK < len(steps):
                        emit_score(i + LOOK)
                    emit_pv(i)
                    if i == MID and hook is not None:
                        hook(); done_hook = True
                if hook is not None and not done_hook:
                    hook()

            def epi_A(n):
                tp = n % 2
                for m in range(2):
                    po, pz = OB[m]
                    k = next_sm()
                    recip(SM[k][:, 0:QT], PS[pz][:, 0:QT], [BPS[pz]], [BSM[k]])
                    tt(om[tp][m][:, 0:QT], PS[po][:, 0:QT], SM[k][:, 0:QT], ALU.mult, [BPS[po], BSM[k]], [Bom[tp][m]])
                stt(om[tp][0][:, 0:QT], om[tp][1][:, 0:QT], neglam[:, l:l + 1], om[tp][0][:, 0:QT], ALU.mult, ALU.add,
                    [Bom[tp][0], Bom[tp][1], Bvec], [Bom[tp][0]])
                q_ = state["sq"] % 2
                state["sq"] += 1
                act(SQ[q_][:, 0:QT], om[tp][0][:, 0:QT], AF.Square, [Bom[tp][0]], [BSQ[q_]])
                return q_

            def epi_B(n, q_):
                ii, qt = tiles_[n]
                t0, seq_idx, h, wt = seq_info(ii)
                tp = n % 2
                q0 = qt * QT
                pb = next_ps()
                mm(PS[pb][:, 0:QT], ones[:], SQ[q_][:, 0:QT], True, True, [BSQ[q_], Bconst], [BPS[pb]])
                act(rs[:, 0:QT], PS[pb][:, 0:QT], AF.Sqrt, [BPS[pb], Bconst], [Brs], bias=epsb[:, 0:1], scale=1.0 / 128)
                recip(rs[:, 0:QT], rs[:, 0:QT], [Brs], [Brs])
                k = next_sm()
                stt(SM[k][:, 0:QT], om[tp][0][:, 0:QT], gsub[:, l:l + 1], rs[:, 0:QT], ALU.mult, ALU.mult, [Bom[tp][0], Bvec, Brs], [BSM[k]])
                dma("sp", o_d[h * 128:(h + 1) * 128, t0 + q0:t0 + q0 + QT], SM[k][:, 0:QT], DSM[k], [BSM[k]], (), acc=[Bo[i] for i in wt])

            head_prep(0)
            q_prep(0)
            pending = None
            for n in range(len(tiles_)):
                if n + 1 < len(tiles_):
                    if tiles_[n + 1][0] != tiles_[n][0]:
                        head_prep(tiles_[n + 1][0])
                    q_prep(n + 1)
                inner(n, pending)
                qq = epi_A(n)
                pending = (lambda n=n, qq=qq: epi_B(n, qq))
            pending()

        def group_of(ti):
            return 1 if ti < 4 else 0

        import os as _os
        KSTOP = _os.environ.get("K_STOP", "")
        if KSTOP == "ffn0":
            load_x_tile(0)
            ffn(0, 1, 0)
            store_y_tile(0)
            S._wait_map("sp", dict(S.out_need))
            S.emit()
            return nc
        if KSTOP == "A0":
            load_x_tile(0)
            phase_A(0, 0, 1)
            store_y_tile(0)
            S.barrier()
            S.emit()
            return nc
        if KSTOP in ("Blru", "Battn", "Bboth", "C0"):
            for ti in range(4):
                load_x_tile(ti)
                phase_A(0, ti, 1)
            S.barrier()
            if KSTOP in ("Blru", "Bboth", "C0"):
                phase_B_lru(0, 0, 2048, 1, True)
                S.barrier()
            if KSTOP in ("Battn", "Bboth", "C0"):
                attn_pipeline(0, [(0, 2048, -1)], True)
                S.barrier()
            if KSTOP == "C0":
                for ti in range(4):
                    phase_C(0, ti, 1)
                    store_y_tile(ti)
            S.barrier()
            S.emit()
            return nc
        if KSTOP in ("Pall", "P1", "P2", "P3"):
            for ti in (4, 5):
                load_x_tile(ti)
                phase_A(0, ti, 0)
            S.barrier()
            if KSTOP == "P1":
                S.emit()
                return nc
            phase_B_lru(0, 2048, 1024, 4, False)
            S.barrier()
            if KSTOP == "P2":
                S.emit()
                return nc
            attn_pipeline(0, [(2048, 1024, 0)], False)
            S.barrier()
            if KSTOP == "P3":
                S.emit()
                return nc
            for ti in (4, 5):
                phase_C(0, ti, 0)
                store_y_tile(ti)
            for hh in range(2):
                pb = next_ps()
                tr(PS[pb][:, 0:128], ssb[:, hh * 128:(hh + 1) * 128], [Bssb], [BPS[pb]])
                k = next_sm()
                vcopy(SM[k][:, 0:128], PS[pb][:, 0:128], [BPS[pb]], [BSM[k]])
                dma("sp", nst[hh * 128:(hh + 1) * 128, :], SM[k][:, 0:128], DSM[k], [BSM[k]], (), final=True)
            S.barrier()
            S.emit()
            return nc
        for l in range(L_RUN):
            for ti in range(NT):
                g = group_of(ti)
                if l == 0:
                    load_x_tile(ti)
                else:
                    phase_C(l - 1, ti, g)
                phase_A(l, ti, g)
            S.barrier()
            phase_B_lru(l, 0, 2048, 1, True)
            phase_B_lru(l, 2048, 1024, 4, False)
            S.barrier()
            attn_pipeline(l, [(0, 2048, -1)], True)
            attn_pipeline(l, [(2048, 1024, 0)], False)
            S.barrier()
        for ti in range(NT):
            g = group_of(ti)
            phase_C(L_RUN - 1, ti, g)
            store_y_tile(ti)
        for hh in range(2):
            pb = next_ps()
            tr(PS[pb][:, 0:128], ssb[:, hh * 128:(hh + 1) * 128], [Bssb], [BPS[pb]])
            k = next_sm()
            vcopy(SM[k][:, 0:128], PS[pb][:, 0:128], [BPS[pb]], [BSM[k]])
            dma("sp", nst[hh * 128:(hh + 1) * 128, :], SM[k][:, 0:128], DSM[k], [BSM[k]], (), final=True)
        S._wait_map("sp", dict(S.out_need))
        S.emit()
        print("built: ops", S.nops, "waits", S.nwait, "dsems", len(S.dsems), flush=True)
    return nc


def _rope_tables():
    GRID_W = 64
    n_tokens = 2048
    t = np.arange(n_tokens)
    row = (t // GRID_W).astype(np.float32)
    col = (t % GRID_W).astype(np.float32)
    freqs = (1.0 / (np.float32(10000.0) ** (np.arange(0, 32, 2, dtype=np.float32) / np.float32(32)))).astype(np.float32)
    ang = np.stack([row[:, None] * freqs, col[:, None] * freqs], axis=1).astype(np.float32)
    cos = np.cos(ang).astype(np.float32)
    sin = np.sin(ang).astype(np.float32)
    C = np.zeros((128, n_tokens), np.float32)
    Sg = np.zeros((128, n_tokens), np.float32)
    P = np.zeros((128, 128), np.float32)
    for p in range(128):
        axis = (p % 64) // 32
        half = (p % 32) // 16
        f = p % 16
        C[p] = cos[:, axis, f]
        Sg[p] = sin[:, axis, f] * (-1.0 if half == 0 else 1.0)
        P[p ^ 16, p] = 1.0
    return C, Sg, P


_CACHE = {}


def kernel(**inputs):
    f = lambda a: np.ascontiguousarray(np.asarray(a, dtype=np.float32))
    if "nc" not in _CACHE:
        _CACHE["nc"] = build_program()
    nc = _CACHE["nc"]
    C, Sg, P = _rope_tables()
    ident = np.eye(128, dtype=np.float32)
    xp = f(inputs["x_prompt"]); xsm = f(inputs["x_sample"]); c = f(inputs["c"]); c_ctx = f(inputs["c_ctx"])
    ck = f(inputs["cache_k"]); cv = f(inputs["cache_v"]); stl = f(inputs["state_lru"])
    shared = {k: f(inputs[k]) for k in ("w_mod", "b_mod", "g_pre", "g_post", "ffn_w1", "ffn_w2", "w_in", "conv_w", "conv_b",
                                        "lru_wa", "lru_ba", "lru_wx", "lru_bx", "lru_lambda", "lam_q1", "lam_k1", "lam_q2",
                                        "lam_k2", "attn_subln", "p_lru", "p_attn", "w_out")}
    shared.update({"c_ident": ident, "c_perm": P, "c_cos": C, "c_sin": Sg})
    in_maps = []
    for b in range(NCORES):
        m = dict(shared)
        m["x_in"] = np.ascontiguousarray(np.concatenate([xsm[b], xp[4 * b:4 * b + 4].reshape(1024, D)], axis=0))
        m["cvec"] = np.ascontiguousarray(np.stack([c_ctx, c[b]], axis=0))
        m["cache_k"] = ck[b]
        m["cache_v"] = cv[b]
        m["state_in"] = stl[b]
        in_maps.append(m)
    res = run_bass_kernel_spmd(nc, in_maps, core_ids=list(range(NCORES)))
    R = res.results
    y_sample = np.stack([R[b]["y_out"][0:2048] for b in range(NCORES)], axis=0)
    y_prompt = np.concatenate([R[b]["y_out"][2048:].reshape(4, 256, D) for b in range(NCORES)], axis=0)
    new_k = np.concatenate([R[b]["nck"].reshape(4, DEPTH, 256, 8, 128) for b in range(NCORES)], axis=0)
    new_v = np.concatenate([R[b]["ncv"].reshape(4, DEPTH, 256, 8, 128) for b in range(NCORES)], axis=0)
    new_s = np.concatenate([R[b]["nst"].reshape(4, DEPTH, 2, 16 * 128) for b in range(NCORES)], axis=0)
    return (y_prompt.astype(np.float32), y_sample.astype(np.float32), new_k.astype(np.float32),
            new_v.astype(np.float32), new_s.astype(np.float32))
```

```python
import contextlib
import math
import numpy as np
import concourse.bass as bass
import concourse.mybir as mybir
from concourse.bass_utils import run_bass_kernel_spmd

F32 = mybir.dt.float32
F32R = mybir.dt.float32r
AF = mybir.ActivationFunctionType
ALU = mybir.AluOpType

D = 2048
KC = 16
T = 512
NTOK = 3072
NT = 6
DFF = 5504
FC = 43
INC = 11264
DEPTH = 2
PAST = 512
EPS = 1e-6
NCORES = 8
FF_GROUPS = [(0, 12), (12, 12), (24, 12), (36, 7)]
L_RUN = DEPTH


class Buf:
    __slots__ = ("name", "w", "r", "excl")

    def __init__(self, name, excl=False):
        self.name = name
        self.w = {}
        self.r = {}
        self.excl = excl


class DSem:
    __slots__ = ("sem", "val", "key")

    def __init__(self, sem, key):
        self.sem = sem
        self.val = 0
        self.key = key


class Sched:
    ENG = ("pe", "act", "dve", "pool", "sp")

    def __init__(self, nc, stack):
        self.nc = nc
        self.stack = stack
        self.ops = {e: [] for e in self.ENG}
        self.sem = {}
        self.cnt = {}
        self.semobj = {}
        for e in ("pe", "act", "dve", "pool"):
            s = stack.enter_context(nc.semaphore("es_" + e))
            self.sem[e] = "es_" + e
            self.semobj["es_" + e] = s
            self.cnt[e] = 0
        self.seen = {e: {} for e in self.ENG}
        self.dsems = []
        self.out_need = {}
        self.nwait = 0
        self.nops = 0

    def dsem(self):
        key = "ds%d" % len(self.dsems)
        s = self.stack.enter_context(self.nc.semaphore(key))
        self.semobj[key] = s
        d = DSem(s, key)
        self.dsems.append(d)
        return d

    def _wait_map(self, eng, need):
        seen = self.seen[eng]
        own = self.sem.get(eng)
        for k, v in need.items():
            if eng == "pe" and k == own:
                continue
            if seen.get(k, 0) >= v:
                continue
            seen[k] = v
            so = self.semobj[k]
            self.ops[eng].append(lambda e, so=so, v=v: e.wait_ge(so, v))
            self.nwait += 1

    def _waits(self, eng, reads, writes):
        need = {}
        for b in reads:
            for k, v in b.w.items():
                if need.get(k, 0) < v:
                    need[k] = v
        for b in writes:
            for d in (b.w, b.r):
                for k, v in d.items():
                    if need.get(k, 0) < v:
                        need[k] = v
        self._wait_map(eng, need)

    def op(self, eng, fn, reads=(), writes=()):
        ex = [b for b in reads if b.excl]
        if ex:
            writes = list(writes) + [b for b in ex if b not in writes]
        self._waits(eng, reads, writes)
        self.cnt[eng] += 1
        k = self.sem[eng]
        v = self.cnt[eng]
        so = self.semobj[k]
        self.ops[eng].append(lambda e, so=so: fn(e).then_inc(so, 1))
        self.nops += 1
        for b in reads:
            if b.r.get(k, 0) < v:
                b.r[k] = v
        for b in writes:
            b.w = {k: v}
            b.r = {}

    def dma(self, eng, fn, ds, reads=(), writes=(), acc=(), out=False):
        self._waits(eng, reads, writes)
        if acc:
            need = {}
            for b in acc:
                for k_, v_ in b.r.items():
                    if need.get(k_, 0) < v_:
                        need[k_] = v_
            self._wait_map(eng, need)
        ds.val += 16
        k, v, so = ds.key, ds.val, ds.sem
        self.ops[eng].append(lambda e, so=so: fn(e).then_inc(so, 16))
        self.nops += 1
        for b in reads:
            if b.r.get(k, 0) < v:
                b.r[k] = v
        for b in writes:
            b.w = {k: v}
            b.r = {}
        for b in acc:
            if b.w.get(k, 0) < v:
                b.w[k] = v
        if out:
            self.out_need[k] = v

    def barrier(self):
        need = {self.sem[e]: self.cnt[e] for e in self.cnt if self.cnt[e] > 0}
        for d in self.dsems:
            if d.val > 0:
                need[d.key] = d.val
        for e in self.ENG:
            n2 = dict(need)
            if e in self.sem:
                n2.pop(self.sem[e], None) if e == "pe" else None
            self._wait_map(e, n2)

    def emit(self):
        nc = self.nc
        with nc.Block() as block:
            @block.sync
            def _(e):
                for f in self.ops["sp"]:
                    f(e)

            @block.tensor
            def _(e):
                for f in self.ops["pe"]:
                    f(e)

            @block.scalar
            def _(e):
                for f in self.ops["act"]:
                    f(e)

            @block.vector
            def _(e):
                for f in self.ops["dve"]:
                    f(e)

            @block.gpsimd
            def _(e):
                for f in self.ops["pool"]:
                    f(e)


def build_program():
    nc = bass.Bass("TRN2", target_bir_lowering=False)

    def din(name, shape):
        return nc.dram_tensor(name, list(shape), F32, kind="ExternalInput").ap()

    def dout(name, shape):
        return nc.dram_tensor(name, list(shape), F32, kind="ExternalOutput").ap()

    import os as _os0
    _DBG = bool(_os0.environ.get("K_DBG", ""))

    def dscr(name, shape):
        if _DBG:
            return nc.dram_tensor(name, list(shape), F32, kind="ExternalOutput").ap()
        return nc.dram_tensor(name, list(shape), F32).ap()

    x_in = din("x_in", [NTOK, D])
    cvec = din("cvec", [2, D])
    cache_k = din("cache_k", [DEPTH, PAST, 8, 128])
    cache_v = din("cache_v", [DEPTH, PAST, 8, 128])
    state_in = din("state_in", [DEPTH, 2, D])
    w_mod = din("w_mod", [DEPTH, D, 9 * D])
    b_mod = din("b_mod", [DEPTH, 9 * D])
    g_pre = din("g_pre", [DEPTH, 3, D])
    g_post = din("g_post", [DEPTH, 3, D])
    ffn_w1 = din("ffn_w1", [DEPTH, 2, D, 2 * DFF])
    ffn_w2 = din("ffn_w2", [DEPTH, 2, DFF, D])
    w_in = din("w_in", [DEPTH, D, INC])
    conv_w = din("conv_w", [DEPTH, 4, D])
    conv_b = din("conv_b", [DEPTH, D])
    lru_wa = din("lru_wa", [DEPTH, 2, 16, 128, 128])
    lru_ba = din("lru_ba", [DEPTH, 2, D])
    lru_wx = din("lru_wx", [DEPTH, 2, 16, 128, 128])
    lru_bx = din("lru_bx", [DEPTH, 2, D])
    lru_lambda = din("lru_lambda", [DEPTH, 2, D])
    lam_q1 = din("lam_q1", [DEPTH, 64])
    lam_k1 = din("lam_k1", [DEPTH, 64])
    lam_q2 = din("lam_q2", [DEPTH, 64])
    lam_k2 = din("lam_k2", [DEPTH, 64])
    attn_subln = din("attn_subln", [DEPTH, 128])
    p_lru = din("p_lru", [DEPTH, D, D])
    p_attn = din("p_attn", [DEPTH, 1024, D])
    w_out = din("w_out", [DEPTH, D, D])
    c_ident = din("c_ident", [128, 128])
    c_perm = din("c_perm", [128, 128])
    c_cos = din("c_cos", [128, 2048])
    c_sin = din("c_sin", [128, 2048])

    y_out = dout("y_out", [NTOK, D])
    nck = dout("nck", [4, DEPTH, 256, 1024])
    ncv = dout("ncv", [4, DEPTH, 256, 1024])
    nst = dout("nst", [256, 128])

    xs_d = dscr("xs_d", [D, NTOK])
    win_d = dscr("win_d", [INC, NTOK])
    rec_d = dscr("rec_d", [D, NTOK])
    o_d = dscr("o_d", [1024, NTOK])

    with contextlib.ExitStack() as st:
        S = Sched(nc, st)

        def sb(name, shape, dt=F32):
            return st.enter_context(nc.sbuf_tensor(name, list(shape), dt))

        def pst(name):
            return st.enter_context(nc.psum_tensor(name, [128, 512], F32))

        ident = sb("ident", [128, 128])
        perm = sb("perm", [128, 128])
        ones = sb("ones", [128, 128], F32R)
        WB = [sb("wb%d" % i, [128, 2, 512], F32R) for i in range(6)]
        BWB = [Buf("wb%d" % i) for i in range(6)]
        DWB = [S.dsem() for _ in range(6)]
        GW = [sb("gw%d" % i, [128, 4, 128], F32R) for i in range(2)]
        BGW = [Buf("gw%d" % i) for i in range(2)]
        DGW = [S.dsem() for _ in range(2)]
        NSM = 6
        SM = [sb("sm%d" % i, [128, 512]) for i in range(NSM)]
        BSM = [Buf("sm%d" % i) for i in range(NSM)]
        DSM = [S.dsem() for _ in range(NSM)]
        SQ = [sb("sq%d" % i, [128, 512], F32R) for i in range(2)]
        BSQ = [Buf("sq%d" % i) for i in range(2)]
        rs = sb("rs", [128, 512]);  Brs = Buf("rs")
        modT = sb("modT", [128, DEPTH, 144, 2]);  BmodT = Buf("modT")
        bmodT = sb("bmodT", [128, DEPTH, 144])
        scT = sb("scT", [128, 16, 2], F32R)
        cT = sb("cT", [128, 2, 16])
        gpreT = sb("gpreT", [128, DEPTH * 3, 16])
        gpostT = sb("gpostT", [128, DEPTH * 3, 16])
        AV = sb("AV", [128, DEPTH, 2, 3, 16])
        GV = sb("GV", [128, DEPTH, 2, 3, 16])
        convT = sb("convT", [128, DEPTH * 4, 16])
        convbT = sb("convbT", [128, DEPTH, 16])
        baT = sb("baT", [128, DEPTH * 2, 16])
        bxT = sb("bxT", [128, DEPTH * 2, 16])
        lamT = sb("lamT", [128, DEPTH * 2, 16])
        cAT = sb("cAT", [128, DEPTH * 2, 16])
        h0T = sb("h0T", [128, DEPTH * 2, 16])
        lamv = sb("lamv", [128, DEPTH, 4, 64])
        lamr = sb("lamr", [128, DEPTH, 4])
        neglam = sb("neglam", [128, DEPTH])
        sublnT = sb("sublnT", [128, DEPTH])
        gsub = sb("gsub", [128, DEPTH])
        ssb = sb("ssb", [128, 256])
        Bssb = Buf("ssb")
        Bvec = Buf("vec")
        Bconst = Buf("const")

        arena = sb("arenaF", [128, 16384])
        arenaR = sb("arenaR", [128, 20480], F32R)

        PS = [pst("ps%d" % i) for i in range(8)]
        BPS = [Buf("ps%d" % i, excl=True) for i in range(8)]

        def av(off, n):
            return arena[:, off:off + n]

        def avr(off, n):
            return arenaR[:, off:off + n]

        xT = av(0, 8192).rearrange("p (c t) -> p c t", c=16)
        yT = av(8192, 8192).rearrange("p (c t) -> p c t", c=16)
        hTr = avr(0, 8192).rearrange("p (c t) -> p c t", c=16)
        actTr = avr(8192, 8192).rearrange("p (c t) -> p c t", c=16)
        oTr = avr(16384, 4096).rearrange("p (c t) -> p c t", c=8)
        BoT = Buf("oT")
        BxT = [Buf("xT%d" % i) for i in range(16)]; BhT = [Buf("hT%d" % i) for i in range(16)]
        ByT = [Buf("yT%d" % i) for i in range(16)]; Bact = [Buf("act%d" % i) for i in range(16)]
        dxT = S.dsem(); dhT = S.dsem(); dyT = S.dsem(); d_oT = S.dsem(); d_xst = S.dsem(); d_yout = S.dsem()

        state = {"wslot": 0, "ps": 0, "sm": 0, "sq": 0, "eb": 0, "bs": 0}
        _PB = {}

        def PBuf(name):
            if name not in _PB:
                _PB[name] = Buf(name)
            return _PB[name]

        PD = [S.dsem() for _ in range(9)]

        def act(out, in_, func, reads, writes, bias=None, scale=None):
            kw = {}
            if bias is not None:
                kw["bias"] = bias
            if scale is not None:
                kw["scale"] = scale
            S.op("act", lambda e: e.activation(out, in_, func, **kw), reads, writes)

        def tt(out, in0, in1, op, reads, writes):
            S.op("dve", lambda e: e.tensor_tensor(out, in0, in1, op), reads, writes)

        def ts(out, in0, s1, s2, op0, op1, reads, writes):
            if op1 is None:
                S.op("dve", lambda e: e.tensor_scalar(out, in0, s1, None, op0), reads, writes)
            else:
                S.op("dve", lambda e: e.tensor_scalar(out, in0, s1, s2, op0, op1), reads, writes)

        def stt(out, in0, scalar, in1, op0, op1, reads, writes):
            S.op("dve", lambda e: e.scalar_tensor_tensor(out, in0, scalar, in1, op0, op1), reads, writes)

        def vcopy(out, in_, reads, writes):
            S.op("dve", lambda e: e.tensor_copy(out, in_), reads, writes)

        def acopy(out, in_, reads, writes):
            act(out, in_, AF.Copy, reads, writes)

        def recip(out, in_, reads, writes):
            S.op("dve", lambda e: e.reciprocal(out, in_), reads, writes)

        def mm(out, lhsT, rhs, start, stop, reads, writes):
            S.op("pe", lambda e: e.matmul(out, lhsT, rhs, start=start, stop=stop), reads, writes)

        def tr(out, in_, reads, writes):
            S.op("pe", lambda e: e.transpose(out, in_, ident[:]), list(reads) + [Bconst], writes)

        def dma(q, out, in_, ds, reads, writes, slow=False, acc=(), final=False):
            if slow:
                S.dma(q, lambda e: e.dma_start(out=out, in_=in_, allow_slow_non_contiguous=True), ds, reads, writes, acc, final)
            else:
                S.dma(q, lambda e: e.dma_start(out=out, in_=in_), ds, reads, writes, acc, final)

        def next_ps(lo=0, hi=4):
            i = lo + state["ps"] % (hi - lo)
            state["ps"] += 1
            return i

        def next_sm():
            i = state["sm"] % NSM
            state["sm"] += 1
            return i

        BANKSETS = ([0, 1, 2, 3], [4, 5, 6, 7])
        def _wview(off):
            return avr(off, 1024).rearrange("p (k c) -> p k c", k=2)
        WPOOL_ALL = [(WB[i][:], BWB[i], DWB[i], None) for i in range(6)]
        WPOOL_ALL += [(_wview(16384 + i * 1024), Buf("wbo%d" % i), S.dsem(), "o") for i in range(4)]
        WPOOL_ALL += [(_wview(8192 + i * 1024), Buf("wba%d" % i), S.dsem(), "a") for i in range(8)]
        BWBX_O = [WPOOL_ALL[6 + i][1] for i in range(4)]
        BWBX_A = [WPOOL_ALL[10 + i][1] for i in range(8)]
        WPOOLS = {"c": list(range(6)), "ffn": list(range(10)) + [16, 17], "win": list(range(18))}
        state["wpool"] = "c"

        def inherit(dst, srcs):
            for b_ in srcs:
                for dd in (b_.w, b_.r):
                    for k_, v_ in dd.items():
                        if dst.r.get(k_, 0) < v_:
                            dst.r[k_] = v_

        def set_wpool(name):
            if name == "ffn":
                for c_ in range(12):
                    inherit(Bact[c_], [BWBX_A[c_ // 2]])
                inherit(BWBX_A[6], Bact[12:14])
                inherit(BWBX_A[7], Bact[14:16])
                for b_ in BWBX_O:
                    inherit(b_, [BoT])
            elif name == "win":
                for i_, b_ in enumerate(BWBX_A):
                    inherit(b_, Bact[2 * i_:2 * i_ + 2])
                for b_ in BWBX_O:
                    inherit(b_, [BoT])
            state["wpool"] = name

        def linear_blocked(Wv, kc0, nk, col0, nch, rhs_fn, rhs_bufs, epilogue, TT=T, bankset=None):
            if bankset is None:
                bankset = BANKSETS[state["bs"] % 2]
                state["bs"] += 1
            cw = nch * 128
            S._waits("pe", [], [BPS[b_] for b_ in bankset[:nch]])
            for kb0 in range(0, nk, 2):
                kn = min(2, nk - kb0)
                pool_ = WPOOLS[state["wpool"]]
                wt_, bw_, dw_, reg_ = WPOOL_ALL[pool_[state["wslot"] % len(pool_)]]
                state["wslot"] += 1
                dma("pool", wt_[:, 0:kn, 0:cw], Wv[:, kc0 + kb0:kc0 + kb0 + kn, col0:col0 + cw], dw_, (), [bw_])
                for i in range(kn):
                    kk = kb0 + i
                    for j in range(nch):
                        pb = bankset[j]
                        mm(PS[pb][:, 0:TT], wt_[:, i, j * 128:(j + 1) * 128], rhs_fn(kk), kk == 0, kk == nk - 1,
                           [bw_] + list(rhs_bufs(kk) if callable(rhs_bufs) else rhs_bufs), [BPS[pb]])
            for j in range(nch):
                epilogue(j, bankset[j])
            return bankset

        dset = S.dsem()
        dset_c = S.dsem()
        dma("sp", ident[:], c_ident, dset_c, (), (), acc=[Bconst])
        dma("sp", perm[:], c_perm, dset_c, (), (), acc=[Bconst])
        onesf = sb("onesf", [128, 128])
        S.op("dve", lambda e: e.memset(onesf[:], 1.0), (), [Bconst])
        S.op("dve", lambda e: e.tensor_copy(ones[:], onesf[:]), [Bconst], [Bconst])

        def vload(dst, src1d):
            dma("sp", dst, src1d.rearrange("(c p) -> p c", p=128), dset, (), (), slow=True, acc=[Bvec])

        for l in range(DEPTH):
            vload(bmodT[:, l, :], b_mod[l])
            vload(convbT[:, l, :], conv_b[l])
            for i in range(3):
                vload(gpreT[:, l * 3 + i, :], g_pre[l, i])
                vload(gpostT[:, l * 3 + i, :], g_post[l, i])
            for j in range(4):
                vload(convT[:, l * 4 + j, :], conv_w[l, j])
            for d_ in range(2):
                vload(baT[:, l * 2 + d_, :], lru_ba[l, d_])
                vload(bxT[:, l * 2 + d_, :], lru_bx[l, d_])
                vload(lamT[:, l * 2 + d_, :], lru_lambda[l, d_])
                vload(h0T[:, l * 2 + d_, :], state_in[l, d_])
            for i, lv in enumerate((lam_q1, lam_k1, lam_q2, lam_k2)):
                dma("sp", lamv[:, l, i, :], lv[l:l + 1, :].broadcast_to([128, 64]), dset, (), (), acc=[Bvec])
            dma("sp", sublnT[:, l:l + 1], attn_subln[l].rearrange("(p o) -> p o", o=1), dset, (), (), slow=True, acc=[Bvec])
        for g in range(2):
            vload(cT[:, g, :], cvec[g])
        for g in range(2):
            act(scT[:, :, g], cT[:, g, :], AF.Silu, [Bvec], [Bvec])
        act(cAT[:], lamT[:], AF.Exp, [Bvec], [Bvec], scale=-1.0)
        act(cAT[:], cAT[:], AF.Ln, [Bvec], [Bvec], bias=1.0)
        ts(cAT[:], cAT[:], -8.0, None, ALU.mult, None, [Bvec], [Bvec])
        for l in range(DEPTH):
            lam_init = 0.8 - 0.6 * math.exp(-0.3 * l)
            for i in range(2):
                tt(lamv[:, l, 2 * i, :], lamv[:, l, 2 * i, :], lamv[:, l, 2 * i + 1, :], ALU.mult, [Bvec], [Bvec])
                S.op("dve", lambda e, l=l, i=i: e.reduce_sum(lamr[:, l, i:i + 1], lamv[:, l, 2 * i, :], axis=mybir.AxisListType.X), [Bvec], [Bvec])
            act(lamr[:, l, 0:2], lamr[:, l, 0:2], AF.Exp, [Bvec], [Bvec])
            stt(neglam[:, l:l + 1], lamr[:, l, 1:2], -lam_init, lamr[:, l, 0:1], ALU.add, ALU.subtract, [Bvec], [Bvec])
            ts(gsub[:, l:l + 1], sublnT[:, l:l + 1], 1.0 - lam_init, None, ALU.mult, None, [Bvec], [Bvec])
        def mod_groups(l):
            wv = w_mod[l].rearrange("(kc p) n -> p kc n", p=128)
            for j4 in range(36):
                def ep_mod(j, pb, j4=j4, l=l):
                    jj = j4 * 4 + j
                    ts(modT[:, l, jj, :], PS[pb][:, 0:2], bmodT[:, l, jj:jj + 1], None, ALU.add, None, [BPS[pb], Bvec], [BmodT])
                linear_blocked(wv, 0, 16, j4 * 512, 4, lambda kk: scT[:, kk, :], [Bvec], ep_mod, TT=2)
                yield
            for g in range(2):
                for i in range(3):
                    coef = 1.0 if i == 1 else 0.5
                    sc = modT[:, l, (3 * i + 1) * 16:(3 * i + 2) * 16, g]
                    gt = modT[:, l, (3 * i + 2) * 16:(3 * i + 3) * 16, g]
                    stt(AV[:, l, g, i, :], sc, 1.0, gpreT[:, l * 3 + i, :], ALU.add, ALU.mult, [BmodT, Bvec], [Bvec])
                    stt(GV[:, l, g, i, :], gt, coef, gpostT[:, l * 3 + i, :], ALU.mult, ALU.mult, [BmodT, Bvec], [Bvec])

        for _ in mod_groups(0):
            pass
        mod_gen = {"g": mod_groups(1) if L_RUN > 1 else None}

        def pump_mod(k):
            g_ = mod_gen["g"]
            if g_ is None:
                return
            for _ in range(k):
                try:
                    next(g_)
                except StopIteration:
                    mod_gen["g"] = None
                    return

        def SH(l, g, i, c):
            return modT[:, l, (3 * i) * 16 + c:(3 * i) * 16 + c + 1, g]

        def rstd_from(src, Bsrc, nch, TT, inv_n):
            for c in range(nch):
                q = state["sq"] % 2
                state["sq"] += 1
                act(SQ[q][:, 0:TT], src(c), AF.Square, [Bsrc[c]], [BSQ[q]])
                mm(PS[4][:, 0:TT], ones[:], SQ[q][:, 0:TT], c == 0, c == nch - 1, [BSQ[q], Bconst], [BPS[4]])
            act(rs[:, 0:TT], PS[4][:, 0:TT], AF.Sqrt, [BPS[4]], [Brs], bias=epsb[:, 0:1], scale=inv_n)
            recip(rs[:, 0:TT], rs[:, 0:TT], [Brs], [Brs])

        epsb = sb("epsb", [128, 1])
        S.op("dve", lambda e: e.memset(epsb[:], EPS), (), [Bconst])

        def normmod(l, g, i):
            rstd_from(lambda c: xT[:, c, :], BxT, 16, T, 1.0 / D)
            for c in range(16):
                k = next_sm()
                stt(SM[k][:], xT[:, c, :], AV[:, l, g, i, c:c + 1], rs[:], ALU.mult, ALU.mult, [BxT[c], Bvec, Brs], [BSM[k]])
                act(hTr[:, c, :], SM[k][:], AF.Identity, [BSM[k], BmodT], [BhT[c]], bias=SH(l, g, i, c))

        def postnorm(l, g, i):
            rstd_from(lambda c: yT[:, c, :], ByT, 16, T, 1.0 / D)
            for c in range(16):
                k = next_sm()
                stt(SM[k][:], yT[:, c, :], GV[:, l, g, i, c:c + 1], rs[:], ALU.mult, ALU.mult, [ByT[c], Bvec, Brs], [BSM[k]])
                tt(xT[:, c, :], xT[:, c, :], SM[k][:], ALU.add, [BSM[k], BxT[c]], [BxT[c]])

        def ffn(l, g, which):
            i = 0 if which == 0 else 2
            normmod(l, g, i)
            set_wpool("ffn")
            w1v = ffn_w1[l, which].rearrange("(kc p) n -> p kc n", p=128)
            w2v = ffn_w2[l, which].rearrange("(kc p) n -> p kc n", p=128)
            actF = actTr.bitcast(F32)
            for gi, (f0, fl) in enumerate(FF_GROUPS):
                for c0 in range(0, fl, 4):
                    nch = min(4, fl - c0)

                    def ep_g(j, pb, c0=c0):
                        act(actTr[:, c0 + j, :], PS[pb][:], AF.Silu, [BPS[pb]], [Bact[c0 + j]])

                    def ep_u(j, pb, c0=c0):
                        tt(actTr[:, c0 + j, :], actF[:, c0 + j, :], PS[pb][:], ALU.mult, [Bact[c0 + j], BPS[pb]], [Bact[c0 + j]])

                    linear_blocked(w1v, 0, 16, (f0 + c0) * 128, nch, lambda kk: hTr[:, kk, :], lambda kk: [BhT[kk]], ep_g, bankset=BANKSETS[0])
                    linear_blocked(w1v, 0, 16, DFF + (f0 + c0) * 128, nch, lambda kk: hTr[:, kk, :], lambda kk: [BhT[kk]], ep_u, bankset=BANKSETS[1])
                for n4 in range(4):
                    def ep_y(j, pb, n4=n4, gi=gi):
                        n = n4 * 4 + j
                        if gi == 0:
                            acopy(yT[:, n, :], PS[pb][:], [BPS[pb]], [ByT[n]])
                        else:
                            tt(yT[:, n, :], PS[pb][:], yT[:, n, :], ALU.add, [BPS[pb], ByT[n]], [ByT[n]])
                    linear_blocked(w2v, f0, fl, n4 * 512, 4, lambda kk: actTr[:, kk, :], lambda kk: [Bact[kk]], ep_y)
            state["wpool"] = "c"
            postnorm(l, g, i)

        def load_x_tile(ti):
            t0 = ti * T
            yv = av(8192, 8192).rearrange("p (b f) -> p b f", b=4)
            dma("sp", yv, x_in[t0:t0 + T, :].rearrange("(b p) f -> p b f", p=128), dyT, (), ByT)
            for c in range(16):
                pb = 5 if c % 2 == 0 else 6
                for b in range(4):
                    tr(PS[pb][:, b * 128:(b + 1) * 128], yv[:, b, c * 128:(c + 1) * 128], ByT[4 * b:4 * b + 4], [BPS[pb]])
                if c % 2 == 0:
                    vcopy(xT[:, c, :], PS[pb][:], [BPS[pb]], [BxT[c]])
                else:
                    acopy(xT[:, c, :], PS[pb][:], [BPS[pb]], [BxT[c]])

        def store_y_tile(ti):
            t0 = ti * T
            yv = av(8192, 8192).rearrange("p (b f) -> p b f", b=4)
            for b in range(4):
                for c4 in range(4):
                    pb = 5 if (b * 4 + c4) % 2 == 0 else 6
                    for cc in range(4):
                        c = c4 * 4 + cc
                        tr(PS[pb][:, cc * 128:(cc + 1) * 128], xT[:, c, b * 128:(b + 1) * 128], [BxT[c]], [BPS[pb]])
                    if c4 % 2 == 0:
                        vcopy(yv[:, b, c4 * 512:(c4 + 1) * 512], PS[pb][:], [BPS[pb]], [ByT[4 * b + c4]])
                    else:
                        acopy(yv[:, b, c4 * 512:(c4 + 1) * 512], PS[pb][:], [BPS[pb]], [ByT[4 * b + c4]])
            dma("sp", y_out[t0:t0 + T, :].rearrange("(b p) f -> p b f", p=128), yv, d_yout, ByT, (), final=True)

        Bout = Buf("out")
        Bwin = [Buf("win%d" % i) for i in range(NT)]
        Bxs = [Buf("xs%d" % i) for i in range(NT)]
        Brec = [Buf("rec%d" % i) for i in range(NT)]
        Bo = [Buf("o%d" % i) for i in range(NT)]

        def phase_A(l, ti, g):
            t0 = ti * T
            ffn(l, g, 0)
            normmod(l, g, 1)
            set_wpool("win")
            wv = w_in[l].rearrange("(kc p) n -> p kc n", p=128)
            for n4 in range(INC // 512):
                def ep_w(j, pb, n4=n4):
                    n = n4 * 4 + j
                    k = next_sm()
                    if n % 2 == 0:
                        vcopy(SM[k][:], PS[pb][:], [BPS[pb]], [BSM[k]])
                    else:
                        acopy(SM[k][:], PS[pb][:], [BPS[pb]], [BSM[k]])
                    dma("sp", win_d[n * 128:(n + 1) * 128, t0:t0 + T], SM[k][:], DSM[k], [BSM[k]], (), acc=[Bwin[ti]])
                linear_blocked(wv, 0, 16, n4 * 512, 4, lambda kk: hTr[:, kk, :], lambda kk: [BhT[kk]], ep_w)
            state["wpool"] = "c"
            dma("sp", xs_d[:, t0:t0 + T].rearrange("(c p) t -> p c t", p=128), xT, d_xst, BxT, [Bxs[ti]])

        def phase_C(l, ti, g):
            t0 = ti * T
            dma("sp", xT, xs_d[:, t0:t0 + T].rearrange("(c p) t -> p c t", p=128), dxT, [Bxs[ti]], BxT)
            dma("pool", actTr, rec_d[:, t0:t0 + T].rearrange("(c p) t -> p c t", p=128), dhT, [Brec[ti]], Bact + BWBX_A)
            dma("pool", oTr, o_d[:, t0:t0 + T].rearrange("(c p) t -> p c t", p=128), d_oT, [Bo[ti]], [BoT] + BWBX_O)
            plv = p_lru[l].rearrange("(kc p) n -> p kc n", p=128)
            pav = p_attn[l].rearrange("(kc p) n -> p kc n", p=128)
            wov = w_out[l].rearrange("(kc p) n -> p kc n", p=128)
            for n4 in range(4):
                def ep_none(j, pb):
                    pass

                def ep_merge(j, p2, n4=n4):
                    n = n4 * 4 + j
                    p1 = BANKSETS[0][j]
                    k1 = next_sm(); k2 = next_sm()
                    dma("sp", SM[k1][:], win_d[7168 + n * 128:7168 + (n + 1) * 128, t0:t0 + T], DSM[k1], [Bwin[ti]], [BSM[k1]])
                    dma("sp", SM[k2][:], win_d[7168 + 2048 + n * 128:7168 + 2048 + (n + 1) * 128, t0:t0 + T], DSM[k2], [Bwin[ti]], [BSM[k2]])
                    act(SM[k1][:], SM[k1][:], AF.Sigmoid, [BSM[k1]], [BSM[k1]])
                    act(SM[k2][:], SM[k2][:], AF.Sigmoid, [BSM[k2]], [BSM[k2]])
                    tt(SM[k1][:], SM[k1][:], PS[p1][:], ALU.mult, [BSM[k1], BPS[p1]], [BSM[k1]])
                    tt(SM[k2][:], SM[k2][:], PS[p2][:], ALU.mult, [BSM[k2], BPS[p2]], [BSM[k2]])
                    tt(hTr[:, n, :], SM[k1][:], SM[k2][:], ALU.add, [BSM[k1], BSM[k2]], [BhT[n]])
                linear_blocked(plv, 0, 16, n4 * 512, 4, lambda kk: actTr[:, kk, :], lambda kk: [Bact[kk]], ep_none, bankset=BANKSETS[0])
                linear_blocked(pav, 0, 8, n4 * 512, 4, lambda kk: oTr[:, kk, :], [BoT], ep_merge, bankset=BANKSETS[1])
            for n4 in range(4):
                def ep_o(j, pb, n4=n4):
                    n = n4 * 4 + j
                    if n % 2 == 0:
                        vcopy(yT[:, n, :], PS[pb][:], [BPS[pb]], [ByT[n]])
                    else:
                        acopy(yT[:, n, :], PS[pb][:], [BPS[pb]], [ByT[n]])
                linear_blocked(wov, 0, 16, n4 * 512, 4, lambda kk: hTr[:, kk, :], lambda kk: [BhT[kk]], ep_o)
            postnorm(l, g, 1)
            ffn(l, g, 1)

        def phase_B_lru(l, t0, SS, nseg, latent):
            L = SS // nseg
            sl_ = [av(i * 2048, SS) for i in range(8)]
            Bs = [PBuf("lru%d" % i) for i in range(8)]
            xr, Bxr = sl_[0], Bs[0]
            gr, Bgr = sl_[0], Bs[0]
            yf, Byf = sl_[1], Bs[1]
            bA = [sl_[2], sl_[4]]; BbA = [Bs[2], Bs[4]]
            bB = [sl_[3], sl_[5]]; BbB = [Bs[3], Bs[5]]
            tm = [sl_[6], sl_[7]]; Btm = [Bs[6], Bs[7]]
            yr = avr(0, SS); Byr = PBuf("yr")
            dl = PD
            tiles = range(t0 // T, (t0 + SS) // T)
            rd = [Bwin[i] for i in tiles]

            def seg(ap, a, b):
                return ap.rearrange("p (s t) -> p s t", s=nseg)[:, :, a:b]

            def ptt(out, in0, in1, op, reads, writes):
                S.op("pool", lambda e: e.tensor_tensor(out, in0, in1, op), reads, writes)

            for n in range(16):
                dma("sp", xr, win_d[n * 128:(n + 1) * 128, t0:t0 + SS], dl[0], rd, [Bxr])
                gs = n % 2
                for d_ in range(2):
                    dma("pool", GW[gs][:, d_ * 2, :], lru_wa[l, d_, n], DGW[gs], (), (), acc=[BGW[gs]])
                    dma("pool", GW[gs][:, d_ * 2 + 1, :], lru_wx[l, d_, n], DGW[gs], (), (), acc=[BGW[gs]])
                cw = lambda j: convT[:, l * 4 + j, n:n + 1]
                act(yf, xr, AF.Identity, [Bxr, Bvec], [Byf], bias=convbT[:, l, n:n + 1], scale=cw(2))
                stt(seg(yf, 2, L), seg(xr, 0, L - 2), cw(0), seg(yf, 2, L), ALU.mult, ALU.add, [Bxr, Bvec, Byf], [Byf])
                stt(seg(yf, 1, L), seg(xr, 0, L - 1), cw(1), seg(yf, 1, L), ALU.mult, ALU.add, [Bxr, Bvec, Byf], [Byf])
                stt(seg(yf, 0, L - 1), seg(xr, 1, L), cw(3), seg(yf, 0, L - 1), ALU.mult, ALU.add, [Bxr, Bvec, Byf], [Byf])
                acopy(yr, yf, [Byf], [Byr])
                dma("sp", gr, win_d[2048 + n * 128:2048 + (n + 1) * 128, t0:t0 + SS], dl[1], rd, [Bgr])
                for d_ in range(2):
                    for blk in range(SS // 512):
                        sl = slice(blk * 512, (blk + 1) * 512)
                        pa = next_ps(); px = next_ps()
                        mm(PS[pa][:], GW[gs][:, d_ * 2, :], yr[:, sl], True, True, [BGW[gs], Byr], [BPS[pa]])
                        mm(PS[px][:], GW[gs][:, d_ * 2 + 1, :], yr[:, sl], True, True, [BGW[gs], Byr], [BPS[px]])
                        act(bA[d_][:, sl], PS[pa][:], AF.Sigmoid, [BPS[pa], Bvec], [BbA[d_]], bias=baT[:, l * 2 + d_, n:n + 1])
                        act(bB[d_][:, sl], PS[px][:], AF.Sigmoid, [BPS[px], Bvec], [BbB[d_]], bias=bxT[:, l * 2 + d_, n:n + 1])
                for d_ in range(2):
                    act(bA[d_], bA[d_], AF.Exp, [BbA[d_], Bvec], [BbA[d_]], scale=cAT[:, l * 2 + d_, n:n + 1])
                for d_ in range(2):
                    act(tm[d_], bA[d_], AF.Square, [BbA[d_]], [Btm[d_]])
                for d_ in range(2):
                    act(tm[d_], tm[d_], AF.Sqrt, [Btm[d_], Bconst], [Btm[d_]], bias=oneb[:, 0:1], scale=-1.0)
                for d_ in range(2):
                    ptt(bB[d_], bB[d_], tm[d_], ALU.mult, [BbB[d_], Btm[d_]], [BbB[d_]])
                for d_ in range(2):
                    tt(bB[d_], bB[d_], yf, ALU.mult, [BbB[d_], Byf], [BbB[d_]])
                for d_ in range(2):
                    for s_ in range(nseg):
                        a0, a1 = s_ * L, (s_ + 1) * L
                        init = h0T[:, l * 2 + d_, n:n + 1] if latent else 0.0
                        if d_ == 0:
                            S.op("dve", lambda e, a0=a0, a1=a1, init=init: e.tensor_tensor_scan(tm[0][:, a0:a1], bA[0][:, a0:a1], bB[0][:, a0:a1], init, ALU.mult, ALU.add),
                                 [BbA[0], BbB[0], Bvec], [Btm[0]])
                        else:
                            def rv(ap, a0=a0, a1=a1):
                                return ap[:, a0:a1][:, ::-1]
                            S.op("dve", lambda e, rv=rv, init=init: e.tensor_tensor_scan(rv(tm[1]), rv(bA[1]), rv(bB[1]), init, ALU.mult, ALU.add),
                                 [BbA[1], BbB[1], Bvec], [Btm[1]])
                    if not latent:
                        hv = tm[d_].rearrange("p (s t) -> p s t", s=nseg)
                        src = hv[:, :, L - 1] if d_ == 0 else hv[:, :, 0]
                        dstv = ssb.rearrange("p (s r) -> p s r", s=4)[:, :, l * 32 + d_ * 16 + n]
                        vcopy(dstv, src, [Btm[d_]], [Bssb])
                ptt(tm[0], tm[0], tm[1], ALU.add, [Btm[0], Btm[1]], [Btm[0]])
                act(bA[0], gr, AF.Square, [Bgr], [BbA[0]])
                ts(bA[0], bA[0], 0.044715, 1.0, ALU.mult, ALU.add, [BbA[0]], [BbA[0]])
                tt(bA[0], bA[0], gr, ALU.mult, [BbA[0], Bgr], [BbA[0]])
                act(bA[0], bA[0], AF.Sigmoid, [BbA[0]], [BbA[0]], scale=1.5957691216057308)
                ptt(tm[0], tm[0], gr, ALU.mult, [Btm[0], Bgr], [Btm[0]])
                tt(tm[1], tm[0], bA[0], ALU.mult, [Btm[0], BbA[0]], [Btm[1]])
                dma("sp", rec_d[n * 128:(n + 1) * 128, t0:t0 + SS], tm[1], dl[2], [Btm[1]], (), acc=[Brec[i] for i in tiles])
                pump_mod(3)

        oneb = sb("oneb", [128, 1])
        S.op("dve", lambda e: e.memset(oneb[:], 1.0), (), [Bconst])

        def attn_pipeline(l, seqs, latent):
            SS = seqs[0][1]
            nkb_c = 4 if latent else 0
            QT = 512 if latent else 256
            nkb = (nkb_c + SS // 128) if latent else (QT // 128)
            cosT = av(0, 2048); sinT = av(2048, 2048)
            kraw = [av(4096, 2048), av(6144, 2048)]
            vraw = [av(8192, 2048), av(10240, 2048)]
            kcs = av(12288, 512).rearrange("p (b d) -> p b d", d=128)
            vcs = av(12800, 512).rearrange("p (b d) -> p b d", d=128)
            qraw = [av(13312, 512), av(13824, 512)]
            om = [[av(14336 + (2 * t_ + m_) * 512, 512) for m_ in range(2)] for t_ in range(2)]
            KT = [avr(0, 2560), avr(11264, 2560)]
            VA = [avr(2560, 2560).rearrange("p (b d) -> p b d", d=128), avr(13824, 2560).rearrange("p (b d) -> p b d", d=128)]
            qz = [[avr(5120 + (2 * t_ + m_) * 512, 512) for m_ in range(2)] for t_ in range(2)]
            NEB = 6
            EB = [avr(8192 + i * 512, 512) for i in range(NEB)]
            Bcs = PBuf("cs")
            BKT = [PBuf("KT0"), PBuf("KT1")]; BVA = [PBuf("VA0"), PBuf("VA1")]
            Bkraw = [PBuf("kraw0"), PBuf("kraw1")]; Bvraw = [PBuf("vraw0"), PBuf("vraw1")]
            Bkcs = PBuf("kcs"); Bvcs = PBuf("vcs")
            Bqraw = [PBuf("qraw0"), PBuf("qraw1")]
            Bqz = [[PBuf("qz%d%d" % (t_, m_)) for m_ in range(2)] for t_ in range(2)]
            Bom = [[PBuf("om%d%d" % (t_, m_)) for m_ in range(2)] for t_ in range(2)]
            BEB = [PBuf("eb%d" % i) for i in range(NEB)]
            dl = PD
            kz = next_sm()
            S.op("dve", lambda e: e.memset(SM[kz][:], 0.0), (), [BSM[kz]])
            for t_ in range(2):
                vcopy(qz[t_][0][64:128, :], SM[kz][64:128, :], [BSM[kz]], [Bqz[t_][0]])
                vcopy(qz[t_][1][0:64, :], SM[kz][0:64, :], [BSM[kz]], [Bqz[t_][1]])
            if latent:
                dma("sp", cosT, c_cos, dl[8], (), (), acc=[Bcs])
                dma("sp", sinT, c_sin, dl[8], (), (), acc=[Bcs])

            items = [(si, h) for si in range(len(seqs)) for h in range(8)]
            tiles_ = [(ii, qt) for ii in range(len(items)) for qt in range(SS // QT)]

            def seq_info(ii):
                t0, _, seq_idx = seqs[items[ii][0]]
                wt = sorted(set(range(t0 // T, (t0 + SS - 1) // T + 1)))
                return t0, seq_idx, items[ii][1], wt

            def rope_to(dsts, src, Bsrc, cols, width):
                pb = next_ps()
                mm(PS[pb][:, 0:width], perm[:], src, True, True, [Bsrc, Bconst], [BPS[pb]])
                k1 = next_sm(); k2 = next_sm()
                tt(SM[k1][:, 0:width], src, cosT[:, cols], ALU.mult, [Bsrc, Bcs], [BSM[k1]])
                tt(SM[k2][:, 0:width], PS[pb][:, 0:width], sinT[:, cols], ALU.mult, [BPS[pb], Bcs], [BSM[k2]])
                for dst, ps_, bd in dsts:
                    tt(dst, SM[k1][ps_, 0:width], SM[k2][ps_, 0:width], ALU.add, [BSM[k1], BSM[k2]], [bd])

            def head_prep(ii):
                t0, seq_idx, h, wt = seq_info(ii)
                hp = ii % 2
                rd = [Bwin[i] for i in wt]
                if latent:
                    dma("sp", kcs, cache_k[l, :, h, :].rearrange("(b p) d -> p b d", p=128), dl[1], (), [Bkcs])
                    pb = next_ps()
                    for b in range(4):
                        tr(PS[pb][:, b * 128:(b + 1) * 128], kcs[:, b, :], [Bkcs], [BPS[pb]])
                    vcopy(KT[hp][:, 0:512], PS[pb][:], [BPS[pb]], [BKT[hp]])
                    dma("sp", vcs, cache_v[l, :, h, :].rearrange("(b p) d -> p b d", p=128), dl[2], (), [Bvcs])
                    acopy(VA[hp][:, 0:4, :], vcs, [Bvcs], [BVA[hp]])
                dma("sp", kraw[hp][:, 0:SS], win_d[5120 + h * 128:5120 + (h + 1) * 128, t0:t0 + SS], dl[3 + hp], rd, [Bkraw[hp]])
                dma("sp", vraw[hp][:, 0:SS], win_d[6144 + h * 128:6144 + (h + 1) * 128, t0:t0 + SS], dl[5 + hp], rd, [Bvraw[hp]])
                if latent:
                    for blk in range(SS // 512):
                        sl = slice(blk * 512, (blk + 1) * 512)
                        rope_to([(KT[hp][:, 512 + blk * 512:512 + (blk + 1) * 512], slice(0, 128), BKT[hp])], kraw[hp][:, sl], Bkraw[hp], sl, 512)
                else:
                    acopy(KT[hp][:, 0:SS], kraw[hp][:, 0:SS], [Bkraw[hp]], [BKT[hp]])
                for b4 in range((SS + 511) // 512):
                    nb = min(4, SS // 128 - b4 * 4)
                    pb = next_ps()
                    for b in range(nb):
                        tb = b4 * 4 + b
                        tr(PS[pb][:, b * 128:(b + 1) * 128], vraw[hp][:, tb * 128:(tb + 1) * 128], [Bvraw[hp]], [BPS[pb]])
                    vcopy(VA[hp][:, nkb_c + b4 * 4:nkb_c + b4 * 4 + nb, :], PS[pb][:, 0:nb * 128].rearrange("p (b d) -> p b d", d=128), [BPS[pb]], [BVA[hp]])
                    if not latent:
                        k = next_sm()
                        acopy(SM[k][:, 0:nb * 128], PS[pb][:, 0:nb * 128], [BPS[pb]], [BSM[k]])
                        s0 = seq_idx + b4 * 2
                        for si_ in range(nb // 2):
                            dma("sp", ncv[s0 + si_, l, :, h * 128:(h + 1) * 128].rearrange("(b p) d -> p b d", p=128),
                                SM[k][:, si_ * 256:(si_ + 1) * 256].rearrange("p (b d) -> p b d", d=128), DSM[k], [BSM[k]], (), final=True)
                if not latent:
                    for b4 in range((SS + 511) // 512):
                        nb = min(4, SS // 128 - b4 * 4)
                        pb = next_ps()
                        for b in range(nb):
                            tb = b4 * 4 + b
                            tr(PS[pb][:, b * 128:(b + 1) * 128], kraw[hp][:, tb * 128:(tb + 1) * 128], [Bkraw[hp]], [BPS[pb]])
                        k = next_sm()
                        vcopy(SM[k][:, 0:nb * 128], PS[pb][:, 0:nb * 128], [BPS[pb]], [BSM[k]])
                        s0 = seq_idx + b4 * 2
                        for si_ in range(nb // 2):
                            dma("sp", nck[s0 + si_, l, :, h * 128:(h + 1) * 128].rearrange("(b p) d -> p b d", p=128),
                                SM[k][:, si_ * 256:(si_ + 1) * 256].rearrange("p (b d) -> p b d", d=128), DSM[k], [BSM[k]], (), final=True)

            def q_prep(n):
                ii, qt = tiles_[n]
                t0, seq_idx, h, wt = seq_info(ii)
                tp = n % 2
                q0 = qt * QT
                rd = [Bwin[i] for i in wt]
                dma("sp", qraw[tp][:, 0:QT], win_d[4096 + h * 128:4096 + (h + 1) * 128, t0 + q0:t0 + q0 + QT], dl[7] if tp else dl[0], rd, [Bqraw[tp]])
                if latent:
                    rope_to([(qz[tp][0][0:64, 0:QT], slice(0, 64), Bqz[tp][0]), (qz[tp][1][64:128, 0:QT], slice(64, 128), Bqz[tp][1])],
                            qraw[tp][:, 0:QT], Bqraw[tp], slice(q0, q0 + QT), QT)
                else:
                    acopy(qz[tp][0][0:64, 0:QT], qraw[tp][0:64, 0:QT], [Bqraw[tp]], [Bqz[tp][0]])
                    acopy(qz[tp][1][64:128, 0:QT], qraw[tp][64:128, 0:QT], [Bqraw[tp]], [Bqz[tp][1]])

            OB = {0: (6, 7), 1: (4, 5)}
            LOOK = 3
            MID = 8

            def inner(n, hook):
                ii, qt = tiles_[n]
                hp = ii % 2; tp = n % 2
                ebuf = {}
                kb0 = 0 if latent else qt * (QT // 128)
                steps = [(kb0 + kb, m) for kb in range(nkb) for m in range(2)]

                def emit_score(i):
                    kb, m = steps[i]
                    pb = next_ps()
                    mm(PS[pb][:, 0:QT], KT[hp][:, kb * 128:(kb + 1) * 128], qz[tp][m][:, 0:QT], True, True,
                       [BKT[hp], Bqz[tp][m]], [BPS[pb]])
                    ei = state["eb"] % NEB
                    state["eb"] += 1
                    act(EB[ei][:, 0:QT], PS[pb][:, 0:QT], AF.Exp, [BPS[pb]], [BEB[ei]], scale=0.125)
                    ebuf[i] = ei

                def emit_pv(i):
                    kb, m = steps[i]
                    ei = ebuf[i]
                    po, pz = OB[m]
                    mm(PS[po][:, 0:QT], VA[hp][:, kb, :], EB[ei][:, 0:QT], kb == kb0, kb == kb0 + nkb - 1, [BVA[hp], BEB[ei]], [BPS[po]])
                    mm(PS[pz][:, 0:QT], ones[:], EB[ei][:, 0:QT], kb == kb0, kb == kb0 + nkb - 1, [Bconst, BEB[ei]], [BPS[pz]])

                for i in range(min(LOOK, len(steps))):
                    emit_score(i)
                done_hook = False
                for i in range(len(steps)):
                    if i + LOOK < len(steps):
                        emit_score(i + LOOK)
                    emit_pv(i)
                    if i == MID and hook is not None:
                        hook(); done_hook = True
                if hook is not None and not done_hook:
                    hook()

            def epi_A(n):
                tp = n % 2
                for m in range(2):
                    po, pz = OB[m]
                    k = next_sm()
                    recip(SM[k][:, 0:QT], PS[pz][:, 0:QT], [BPS[pz]], [BSM[k]])
                    tt(om[tp][m][:, 0:QT], PS[po][:, 0:QT], SM[k][:, 0:QT], ALU.mult, [BPS[po], BSM[k]], [Bom[tp][m]])
                stt(om[tp][0][:, 0:QT], om[tp][1][:, 0:QT], neglam[:, l:l + 1], om[tp][0][:, 0:QT], ALU.mult, ALU.add,
                    [Bom[tp][0], Bom[tp][1], Bvec], [Bom[tp][0]])
                q_ = state["sq"] % 2
                state["sq"] += 1
                act(SQ[q_][:, 0:QT], om[tp][0][:, 0:QT], AF.Square, [Bom[tp][0]], [BSQ[q_]])
                return q_

            def epi_B(n, q_):
                ii, qt = tiles_[n]
                t0, seq_idx, h, wt = seq_info(ii)
                tp = n % 2
                q0 = qt * QT
                pb = next_ps()
                mm(PS[pb][:, 0:QT], ones[:], SQ[q_][:, 0:QT], True, True, [BSQ[q_], Bconst], [BPS[pb]])
                act(rs[:, 0:QT], PS[pb][:, 0:QT], AF.Sqrt, [BPS[pb], Bconst], [Brs], bias=epsb[:, 0:1], scale=1.0 / 128)
                recip(rs[:, 0:QT], rs[:, 0:QT], [Brs], [Brs])
                k = next_sm()
                stt(SM[k][:, 0:QT], om[tp][0][:, 0:QT], gsub[:, l:l + 1], rs[:, 0:QT], ALU.mult, ALU.mult, [Bom[tp][0], Bvec, Brs], [BSM[k]])
                dma("sp", o_d[h * 128:(h + 1) * 128, t0 + q0:t0 + q0 + QT], SM[k][:, 0:QT], DSM[k], [BSM[k]], (), acc=[Bo[i] for i in wt])

            head_prep(0)
            q_prep(0)
            pending = None
            for n in range(len(tiles_)):
                if n + 1 < len(tiles_):
                    if tiles_[n + 1][0] != tiles_[n][0]:
                        head_prep(tiles_[n + 1][0])
                    q_prep(n + 1)
                inner(n, pending)
                qq = epi_A(n)
                pending = (lambda n=n, qq=qq: epi_B(n, qq))
            pending()

        def group_of(ti):
            return 1 if ti < 4 else 0

        import os as _os
        KSTOP = _os.environ.get("K_STOP", "")
        if KSTOP == "ffn0":
            load_x_tile(0)
            ffn(0, 1, 0)
            store_y_tile(0)
            S._wait_map("sp", dict(S.out_need))
            S.emit()
            return nc
        if KSTOP == "A0":
            load_x_tile(0)
            phase_A(0, 0, 1)
            store_y_tile(0)
            S.barrier()
            S.emit()
            return nc
        if KSTOP in ("Blru", "Battn", "Bboth", "C0"):
            for ti in range(4):
                load_x_tile(ti)
                phase_A(0, ti, 1)
            S.barrier()
            if KSTOP in ("Blru", "Bboth", "C0"):
                phase_B_lru(0, 0, 2048, 1, True)
                S.barrier()
            if KSTOP in ("Battn", "Bboth", "C0"):
                attn_pipeline(0, [(0, 2048, -1)], True)
                S.barrier()
            if KSTOP == "C0":
                for ti in range(4):
                    phase_C(0, ti, 1)
                    store_y_tile(ti)
            S.barrier()
            S.emit()
            return nc
        if KSTOP in ("Pall", "P1", "P2", "P3"):
            for ti in (4, 5):
                load_x_tile(ti)
                phase_A(0, ti, 0)
            S.barrier()
            if KSTOP == "P1":
                S.emit()
                return nc
            phase_B_lru(0, 2048, 1024, 4, False)
            S.barrier()
            if KSTOP == "P2":
                S.emit()
                return nc
            attn_pipeline(0, [(2048, 1024, 0)], False)
            S.barrier()
            if KSTOP == "P3":
                S.emit()
                return nc
            for ti in (4, 5):
                phase_C(0, ti, 0)
                store_y_tile(ti)
            for hh in range(2):
                pb = next_ps()
                tr(PS[pb][:, 0:128], ssb[:, hh * 128:(hh + 1) * 128], [Bssb], [BPS[pb]])
                k = next_sm()
                vcopy(SM[k][:, 0:128], PS[pb][:, 0:128], [BPS[pb]], [BSM[k]])
                dma("sp", nst[hh * 128:(hh + 1) * 128, :], SM[k][:, 0:128], DSM[k], [BSM[k]], (), final=True)
            S.barrier()
            S.emit()
            return nc
        for l in range(L_RUN):
            for ti in range(NT):
                g = group_of(ti)
                if l == 0:
                    load_x_tile(ti)
                else:
                    phase_C(l - 1, ti, g)
                phase_A(l, ti, g)
            S.barrier()
            phase_B_lru(l, 0, 2048, 1, True)
            phase_B_lru(l, 2048, 1024, 4, False)
            S.barrier()
            attn_pipeline(l, [(0, 2048, -1)], True)
            attn_pipeline(l, [(2048, 1024, 0)], False)
            S.barrier()
        for ti in range(NT):
            g = group_of(ti)
            phase_C(L_RUN - 1, ti, g)
            store_y_tile(ti)
        for hh in range(2):
            pb = next_ps()
            tr(PS[pb][:, 0:128], ssb[:, hh * 128:(hh + 1) * 128], [Bssb], [BPS[pb]])
            k = next_sm()
            vcopy(SM[k][:, 0:128], PS[pb][:, 0:128], [BPS[pb]], [BSM[k]])
            dma("sp", nst[hh * 128:(hh + 1) * 128, :], SM[k][:, 0:128], DSM[k], [BSM[k]], (), final=True)
        S._wait_map("sp", dict(S.out_need))
        S.emit()
        print("built: ops", S.nops, "waits", S.nwait, "dsems", len(S.dsems), flush=True)
    return nc


def _rope_tables():
    GRID_W = 64
    n_tokens = 2048
    t = np.arange(n_tokens)
    row = (t // GRID_W).astype(np.float32)
    col = (t % GRID_W).astype(np.float32)
    freqs = (1.0 / (np.float32(10000.0) ** (np.arange(0, 32, 2, dtype=np.float32) / np.float32(32)))).astype(np.float32)
    ang = np.stack([row[:, None] * freqs, col[:, None] * freqs], axis=1).astype(np.float32)
    cos = np.cos(ang).astype(np.float32)
    sin = np.sin(ang).astype(np.float32)
    C = np.zeros((128, n_tokens), np.float32)
    Sg = np.zeros((128, n_tokens), np.float32)
    P = np.zeros((128, 128), np.float32)
    for p in range(128):
        axis = (p % 64) // 32
        half = (p % 32) // 16
        f = p % 16
        C[p] = cos[:, axis, f]
        Sg[p] = sin[:, axis, f] * (-1.0 if half == 0 else 1.0)
        P[p ^ 16, p] = 1.0
    return C, Sg, P


_CACHE = {}


def kernel(**inputs):
    f = lambda a: np.ascontiguousarray(np.asarray(a, dtype=np.float32))
    if "nc" not in _CACHE:
        _CACHE["nc"] = build_program()
    nc = _CACHE["nc"]
    C, Sg, P = _rope_tables()
    ident = np.eye(128, dtype=np.float32)
    xp = f(inputs["x_prompt"]); xsm = f(inputs["x_sample"]); c = f(inputs["c"]); c_ctx = f(inputs["c_ctx"])
    ck = f(inputs["cache_k"]); cv = f(inputs["cache_v"]); stl = f(inputs["state_lru"])
    shared = {k: f(inputs[k]) for k in ("w_mod", "b_mod", "g_pre", "g_post", "ffn_w1", "ffn_w2", "w_in", "conv_w", "conv_b",
                                        "lru_wa", "lru_ba", "lru_wx", "lru_bx", "lru_lambda", "lam_q1", "lam_k1", "lam_q2",
                                        "lam_k2", "attn_subln", "p_lru", "p_attn", "w_out")}
    shared.update({"c_ident": ident, "c_perm": P, "c_cos": C, "c_sin": Sg})
    in_maps = []
    for b in range(NCORES):
        m = dict(shared)
        m["x_in"] = np.ascontiguousarray(np.concatenate([xsm[b], xp[4 * b:4 * b + 4].reshape(1024, D)], axis=0))
        m["cvec"] = np.ascontiguousarray(np.stack([c_ctx, c[b]], axis=0))
        m["cache_k"] = ck[b]
        m["cache_v"] = cv[b]
        m["state_in"] = stl[b]
        in_maps.append(m)
    res = run_bass_kernel_spmd(nc, in_maps, core_ids=list(range(NCORES)))
    R = res.results
    y_sample = np.stack([R[b]["y_out"][0:2048] for b in range(NCORES)], axis=0)
    y_prompt = np.concatenate([R[b]["y_out"][2048:].reshape(4, 256, D) for b in range(NCORES)], axis=0)
    new_k = np.concatenate([R[b]["nck"].reshape(4, DEPTH, 256, 8, 128) for b in range(NCORES)], axis=0)
    new_v = np.concatenate([R[b]["ncv"].reshape(4, DEPTH, 256, 8, 128) for b in range(NCORES)], axis=0)
    new_s = np.concatenate([R[b]["nst"].reshape(4, DEPTH, 2, 16 * 128) for b in range(NCORES)], axis=0)
    return (y_prompt.astype(np.float32), y_sample.astype(np.float32), new_k.astype(np.float32),
            new_v.astype(np.float32), new_s.astype(np.float32))
```
